# Optimizing a Trainium2 kernel written in Bass

```python
import jax, jax.numpy as jnp
from jax import lax
import numpy as np

D_MODEL = 1024
BATCH = 8
SEQ = 2048
DEPTH = 2

CHUNK = 64
Q_BLOCK = 128
N_A = DEPTH // 2
N_B = DEPTH - N_A
CONV_W = 3
N_HEADS = 8
QK_NOPE = 128
QK_ROPE = 64
V_HEAD = 128
Q_LORA = 384
KV_LORA = 256
D_FF = 2816
ROPE_THETA = 10000.0
EPS = 1e-6
NEG_INF = -1e30
MAX_POS_OFFSET = 8192

kernel_name = 'yoco_shortconv_mla_convffn'


def rms_norm(x, g):
    xf = x.astype(jnp.float32)
    y = xf * lax.rsqrt(jnp.mean(xf * xf, axis=-1, keepdims=True) + EPS)
    return (y * g.astype(jnp.float32)).astype(x.dtype)


def causal_dwconv(x, w):
    s = x.shape[1]
    xp = jnp.pad(x, ((0, 0), (CONV_W - 1, 0), (0, 0)))
    y = xp[:, 0:s, :] * w[0]
    for j in range(1, CONV_W):
        y = y + xp[:, j:j + s, :] * w[j]
    return y


def rope(x, positions):
    half = QK_ROPE // 2
    inv_freq = 1.0 / (ROPE_THETA ** (jnp.arange(half, dtype=jnp.float32) / half))
    ang = positions.astype(jnp.float32)[..., None] * inv_freq
    cos, sin = jnp.cos(ang), jnp.sin(ang)
    if x.ndim == 4:
        cos, sin = cos[:, :, None, :], sin[:, :, None, :]
    x1 = x[..., :half].astype(jnp.float32)
    x2 = x[..., half:].astype(jnp.float32)
    out = jnp.concatenate([x1 * cos - x2 * sin, x2 * cos + x1 * sin], axis=-1)
    return out.astype(x.dtype)


def short_conv_mixer(h, w_in, conv_w, w_out):
    b_gate, c_gate, u = jnp.split(h @ w_in, 3, axis=-1)
    return (b_gate * causal_dwconv(c_gate * u, conv_w)) @ w_out


def conv_ffn(h, w_up, conv_w, conv_b, w_down):
    g, v = jnp.split(h @ w_up, 2, axis=-1)
    g = causal_dwconv(g, conv_w) + conv_b
    return (jax.nn.silu(g) * v) @ w_down


def shared_kv(h, kv_in_norm, w_dkv, kv_latent_norm, w_kr, w_uk, w_uv, positions):
    b, s, _ = h.shape
    hn = rms_norm(h, kv_in_norm)
    c_kv = rms_norm(hn @ w_dkv, kv_latent_norm)
    k_rope = rope(hn @ w_kr, positions)
    k_nope = (c_kv @ w_uk).reshape(b, s, N_HEADS, QK_NOPE)
    v = (c_kv @ w_uv).reshape(b, s, N_HEADS, V_HEAD)
    return k_nope, k_rope, v


def mla_attention(h, w_dq, q_latent_norm, w_uq, w_o, k_nope, k_rope, v, positions):
    b, s, _ = h.shape
    c_q = rms_norm(h @ w_dq, q_latent_norm)
    q = (c_q @ w_uq).reshape(b, s, N_HEADS, QK_NOPE + QK_ROPE)
    q_nope = q[..., :QK_NOPE]
    q_rope = rope(q[..., QK_NOPE:], positions)
    scale = (QK_NOPE + QK_ROPE) ** -0.5
    nb = s // Q_BLOCK
    qn_blocks = q_nope.reshape(b, nb, Q_BLOCK, N_HEADS, QK_NOPE).transpose(1, 0, 2, 3, 4)
    qr_blocks = q_rope.reshape(b, nb, Q_BLOCK, N_HEADS, QK_ROPE).transpose(1, 0, 2, 3, 4)
    key_chunk = jnp.arange(s) // CHUNK

    def attend_block(args):
        qn, qr, blk = args
        sc = (jnp.einsum('bqhd,bkhd->bhqk', qn, k_nope)
              + jnp.einsum('bqhr,bkr->bhqk', qr, k_rope)).astype(jnp.float32) * scale
        q_chunk = (blk * Q_BLOCK + jnp.arange(Q_BLOCK)) // CHUNK
        mask = key_chunk[None, :] <= q_chunk[:, None]
        sc = jnp.where(mask[None, None], sc, NEG_INF)
        p = jax.nn.softmax(sc, axis=-1).astype(v.dtype)
        return jnp.einsum('bhqk,bkhd->bqhd', p, v)

    o = lax.map(attend_block, (qn_blocks, qr_blocks, jnp.arange(nb)))
    o = o.transpose(1, 0, 2, 3, 4).reshape(b, s, N_HEADS * V_HEAD)
    return o @ w_o


def setup_inputs(seed: int = 0) -> dict:
    key = jax.random.key(seed)
    ks = jax.random.split(key, 32)
    f32 = jnp.float32
    resid = (2 * DEPTH) ** -0.5

    def w(k, shape, fan_in, extra=1.0):
        return jax.random.normal(k, shape, f32) * (fan_in ** -0.5) * extra

    def gain(k, shape):
        return 1.0 + 0.02 * jax.random.normal(k, shape, f32)

    x = jax.random.normal(ks[0], (BATCH, SEQ, D_MODEL), f32)
    offsets = jax.random.randint(ks[1], (BATCH, 1), 0, MAX_POS_OFFSET, dtype=jnp.int32)
    positions = offsets + jnp.arange(SEQ, dtype=jnp.int32)[None, :]
    return {
        'x': x,
        'positions': positions,
        'attn_norm': gain(ks[2], (DEPTH, D_MODEL)),
        'ffn_norm': gain(ks[3], (DEPTH, D_MODEL)),
        'final_norm': gain(ks[4], (D_MODEL,)),
        'sc_w_in': w(ks[5], (N_A, D_MODEL, 3 * D_MODEL), D_MODEL),
        'sc_conv_w': w(ks[6], (N_A, CONV_W, D_MODEL), CONV_W),
        'sc_w_out': w(ks[7], (N_A, D_MODEL, D_MODEL), D_MODEL, resid),
        'kv_in_norm': gain(ks[8], (D_MODEL,)),
        'w_dkv': w(ks[9], (D_MODEL, KV_LORA), D_MODEL),
        'kv_latent_norm': gain(ks[10], (KV_LORA,)),
        'w_kr': w(ks[11], (D_MODEL, QK_ROPE), D_MODEL),
        'w_uk': w(ks[12], (KV_LORA, N_HEADS * QK_NOPE), KV_LORA),
        'w_uv': w(ks[13], (KV_LORA, N_HEADS * V_HEAD), KV_LORA),
        'w_dq': w(ks[14], (N_B, D_MODEL, Q_LORA), D_MODEL),
        'q_latent_norm': gain(ks[15], (N_B, Q_LORA)),
        'w_uq': w(ks[16], (N_B, Q_LORA, N_HEADS * (QK_NOPE + QK_ROPE)), Q_LORA),
        'w_o': w(ks[17], (N_B, N_HEADS * V_HEAD, D_MODEL), N_HEADS * V_HEAD, resid),
        'ffn_w_up': w(ks[18], (DEPTH, D_MODEL, 2 * D_FF), D_MODEL),
        'ffn_conv_w': w(ks[19], (DEPTH, CONV_W, D_FF), CONV_W),
        'ffn_conv_b': 0.02 * jax.random.normal(ks[20], (DEPTH, D_FF), f32),
        'ffn_w_down': w(ks[21], (DEPTH, D_FF, D_MODEL), D_FF, resid),
    }


def reference(x, positions, attn_norm, ffn_norm, final_norm, sc_w_in, sc_conv_w, sc_w_out,
              kv_in_norm, w_dkv, kv_latent_norm, w_kr, w_uk, w_uv,
              w_dq, q_latent_norm, w_uq, w_o,
              ffn_w_up, ffn_conv_w, ffn_conv_b, ffn_w_down):
    h = x
    kv = None
    for layer in range(DEPTH):
        hn = rms_norm(h, attn_norm[layer])
        if layer < N_A:
            h = h + short_conv_mixer(hn, sc_w_in[layer], sc_conv_w[layer], sc_w_out[layer])
        else:
            i = layer - N_A
            k_nope, k_rope, v = kv
            h = h + mla_attention(hn, w_dq[i], q_latent_norm[i], w_uq[i], w_o[i],
                                  k_nope, k_rope, v, positions)
        h = h + conv_ffn(rms_norm(h, ffn_norm[layer]), ffn_w_up[layer], ffn_conv_w[layer],
                         ffn_conv_b[layer], ffn_w_down[layer])
        if layer == N_A - 1:
            kv = shared_kv(h, kv_in_norm, w_dkv, kv_latent_norm, w_kr, w_uk, w_uv, positions)
    return rms_norm(h, final_norm)
```

```python
import contextlib
import math
import numpy as np
import concourse.bass as bass
import concourse.mybir as mybir
from concourse.bass_utils import run_bass_kernel_spmd

F32 = mybir.dt.float32
BF16 = mybir.dt.bfloat16
I32 = mybir.dt.int32
ALU = mybir.AluOpType
AF = mybir.ActivationFunctionType

S = 2048
D = 1024
NTT = 4
TT = 512
DFF = 2816
NHC = 22
GRP = 4
NHEAD = 8
EPS = 1e-6
SCALE = 192.0 ** -0.5
TWO_PI = 2.0 * math.pi

C_ATTN = 0
C_FFN = 16
C_FINAL = 32
C_KVIN = 40
C_KVLAT = 48
C_QLAT = 50
C_SCW = 56
C_FCW = 80
C_FCB = 212
C_INVF = 256
C_SIGN = 257
C_BIAS = 258
C_NCOL = 264

SAME_ENG_SYNC = True


class Res:
    __slots__ = ("name", "last_w", "readers")

    def __init__(self, name):
        self.name = name
        self.last_w = None
        self.readers = []


class Op:
    __slots__ = ("eng", "fn", "deps", "dma", "signal", "seq", "waits", "idx")


class Prog:
    ENGS = ("pe", "act", "dve", "pool", "sp")

    def __init__(self):
        self.ops = []

    def add(self, eng, fn, r=(), w=(), dma=None):
        op = Op()
        op.eng = eng
        op.fn = fn
        op.dma = dma
        op.idx = len(self.ops)
        op.signal = False
        op.seq = None
        deps = {}
        for res in r:
            if res.last_w is not None:
                deps[res.last_w] = True
        for res in w:
            if res.last_w is not None:
                deps[res.last_w] = True
            for rd in res.readers:
                deps.setdefault(rd, False)
        deps.pop(op.idx, None)
        op.deps = deps
        for res in r:
            res.readers.append(op.idx)
        for res in w:
            res.last_w = op.idx
            res.readers = []
        self.ops.append(op)
        return op

    def emit(self, nc, block, stack):
        ops = self.ops
        dma_last = {}
        dma_cnt = {}
        for op in ops:
            if op.dma is not None:
                prev = dma_last.get(op.dma)
                if prev is not None:
                    op.deps[prev] = True
                dma_last[op.dma] = op.idx
                dma_cnt[op.dma] = dma_cnt.get(op.dma, 0) + 16
                op.seq = dma_cnt[op.dma]
        need = []
        for op in ops:
            lst = []
            for d, strong in op.deps.items():
                y = ops[d]
                if y.dma is not None:
                    lst.append(d)
                    continue
                if y.eng == op.eng and op.dma is None:
                    if y.eng == "pe":
                        continue
                    if not (SAME_ENG_SYNC and strong):
                        continue
                y.signal = True
                lst.append(d)
            need.append(lst)
        cnt = {e: 0 for e in self.ENGS}
        for op in ops:
            if op.dma is None and op.signal:
                cnt[op.eng] += 1
                op.seq = cnt[op.eng]
        esem = {e: stack.enter_context(nc.semaphore("s_" + e)) for e in self.ENGS}
        dsem = {k: stack.enter_context(nc.semaphore("d_%s" % (k,))) for k in dma_cnt}
        waited = {e: {} for e in self.ENGS}
        per_eng = {e: [] for e in self.ENGS}
        for op, lst in zip(ops, need):
            ws = {}
            for d in lst:
                y = ops[d]
                key = ("d", y.dma) if y.dma is not None else ("e", y.eng)
                ws[key] = max(ws.get(key, 0), y.seq)
            out = []
            for key, val in ws.items():
                if waited[op.eng].get(key, 0) >= val:
                    continue
                waited[op.eng][key] = val
                out.append((dsem[key[1]] if key[0] == "d" else esem[key[1]], val))
            op.waits = out
            per_eng[op.eng].append(op)

        def run(eng_name):
            def body(e):
                for op in per_eng[eng_name]:
                    for sem, val in op.waits:
                        e.wait_ge(sem, val)
                    if op.fn is None:
                        continue
                    ins = op.fn(e)
                    if op.dma is not None:
                        ins.then_inc(dsem[op.dma], 16)
                    elif op.signal:
                        ins.then_inc(esem[eng_name], 1)
            return body

        block.tensor(run("pe"))
        block.scalar(run("act"))
        block.vector(run("dve"))
        block.gpsimd(run("pool"))
        block.sync(run("sp"))
        self.stats = {e: len(per_eng[e]) for e in self.ENGS}


def grid(name, *dims):
    if len(dims) == 1:
        return [Res("%s%d" % (name, i)) for i in range(dims[0])]
    return [grid("%s%d_" % (name, i), *dims[1:]) for i in range(dims[0])]


def build(stop_after=None, final_norm=True):
    nc = bass.Bass("TRN2", target_bir_lowering=False)
    prog = Prog()
    add = prog.add

    def dram(name, shape, dt, kind="ExternalInput"):
        return nc.dram_tensor(name, shape, dt, kind=kind).ap()

    xT_d = dram("xT", [128, 8 * S], F32)
    pos_d = dram("pos", [128, S], I32)
    cst_d = dram("cst", [128, C_NCOL], F32)
    cb_d = dram("cb", [128, 256], F32)
    win_d = dram("w_in", [8, 128, 8 * 3 * 128], F32)
    wout_d = dram("w_out", [8, 128, 8 * 128], F32)
    wup_d = dram("w_up", [2, NHC, 128, 8 * 2 * 128], F32)
    wdn_d = dram("w_down", [2, NHC, 128, 1024], F32)
    wdkv_d = dram("w_dkv", [128, 8 * 256], F32)
    wkr_d = dram("w_kr", [128, 8 * 128], F32)
    wuk_d = dram("w_uk", [128, 2 * 1024], F32)
    wuv_d = dram("w_uv", [128, 2 * 1024], F32)
    wdq_d = dram("w_dq", [128, 8 * 384], F32)
    wuq_d = dram("w_uq", [8, 128, 3 * 256], F32)
    wo_d = dram("w_o", [8, 128, 8 * 128], F32)
    out_d = dram("outT", [128, 8 * S], F32, kind="ExternalOutput")

    with contextlib.ExitStack() as stack:
        def sb(name, shape, dt):
            return stack.enter_context(nc.sbuf_tensor("sb_" + name, shape, dt))

        h = sb("h", [128, 8 * S], F32)
        A = sb("A", [128, 8 * S], BF16)
        B = sb("B", [128, 8 * S], BF16)
        W1 = sb("W1", [128, 2 * 1024], BF16)
        W2 = sb("W2", [128, 6144], BF16)
        W3 = sb("W3", [128, 6144], BF16)
        W4 = sb("W4", [128, 8192], BF16)
        Dg = sb("Dg", [128, 2050], F32)
        E = sb("E", [128, 10752], BF16)
        cst = sb("cst", [128, C_NCOL], F32)
        cb = sb("cb", [128, 256], BF16)
        sq = sb("sq", [128, 2 * TT], BF16)
        rs = sb("rs", [128, TT], F32)
        scr = sb("scr", [128, 8], F32)
        banks = [stack.enter_context(nc.psum_tensor("ps%d" % i, [128, TT], F32)) for i in range(8)]

        Ef32 = E[:, :].bitcast(F32)
        Ei32 = E[:, :].bitcast(I32)
        Bf32 = B[:, :].bitcast(F32)

        rs2 = sq[:, :].bitcast(F32)
        ones = cb[:, 0:128]
        Jm = cb[:, 128:256]

        H = grid("H", 8, 4)
        RA = grid("A", 8, 4)
        PS = grid("PS", 8)
        SQ = grid("SQ", 2)
        RS = Res("RS")
        CST = Res("CST")
        SCR = Res("SCR")
        OUT = grid("OUT", 8)

        class BankPool:
            def __init__(self, ids):
                self.ids = list(ids)
                self.i = 0

            def next(self):
                b = self.ids[self.i % len(self.ids)]
                self.i += 1
                return b

        allb = BankPool(range(8))

        def hs(c, tt):
            return h[:, c * S + tt * TT: c * S + (tt + 1) * TT]

        def As(c, tt):
            return A[:, c * S + tt * TT: c * S + (tt + 1) * TT]

        def Bs(r, tt):
            return B[:, r * S + tt * TT: r * S + (tt + 1) * TT]

        def ccol(i):
            return cst[:, i:i + 1]

        def handoff(old, new):
            add("dve", lambda e: e.memset(scr[:, 0:1], 0.0), r=(), w=list(old) + list(new) + [SCR])

        def wdma(out_ap, in_ap, wres, key):
            add("pool", lambda e: e.dma_start(out=out_ap, in_=in_ap, max_dma_last_dim=4096), r=(), w=wres, dma=key)

        def mm_group(bank, pairs, rres):
            n = len(pairs)

            def fn(e):
                ins = None
                for i, (o, l, r) in enumerate(pairs):
                    ins = e.matmul(o, l, r, start=(i == 0), stop=(i == n - 1))
                return ins
            add("pe", fn, r=rres, w=[PS[bank]])

        add("sp", lambda e: e.dma_start(out=cst[:, :], in_=cst_d[:, :]), w=[CST], dma="cst")
        wdma(cb[:, :], cb_d[:, :], [CST], "cb")
        h3 = h[:, :].rearrange("p (c t) -> p c t", c=8)
        x3 = xT_d[:, :].rearrange("p (c t) -> p c t", c=8)
        o3 = out_d[:, :].rearrange("p (c t) -> p c t", c=8)
        for tt in range(NTT):
            add("sp", lambda e, tt=tt: e.dma_start(out=h3[:, :, tt * TT:(tt + 1) * TT], in_=x3[:, :, tt * TT:(tt + 1) * TT]),
                w=[H[c][tt] for c in range(8)], dma="x%d" % tt)

        def rmsnorm(nch, src, src_res, gcol, dst, dst_res, nfeat):
            for tt in range(NTT):
                bank = allb.next()
                for c in range(nch):
                    s_ = c % 2
                    add("act", lambda e, c=c, s_=s_, tt=tt: e.activation(
                        out=sq[:, s_ * TT:(s_ + 1) * TT], in_=src(c, tt), func=AF.Square),
                        r=[src_res(c, tt)], w=[SQ[s_]])
                    add("pe", lambda e, c=c, s_=s_, bank=bank: e.matmul(
                        banks[bank][:, :], ones, sq[:, s_ * TT:(s_ + 1) * TT],
                        start=(c == 0), stop=(c == nch - 1)),
                        r=[SQ[s_], CST], w=[PS[bank]])
                add("act", lambda e, bank=bank: e.activation(
                    out=rs[:, :], in_=banks[bank][:, :], func=AF.Ln, bias=ccol_eps, scale=1.0 / nfeat),
                    r=[PS[bank], EPSC], w=[RS])
                add("act", lambda e: e.activation(out=rs[:, :], in_=rs[:, :], func=AF.Exp, scale=-0.5),
                    r=[RS], w=[RS])
                for c in range(nch):
                    add("dve", lambda e, c=c, tt=tt: e.scalar_tensor_tensor(
                        out=dst(c, tt), in0=src(c, tt), scalar=ccol(gcol + c), in1=rs[:, :],
                        op0=ALU.mult, op1=ALU.mult),
                        r=[src_res(c, tt), RS, CST], w=[dst_res(c, tt)])

        EPSC = Res("EPSC")
        add("dve", lambda e: e.memset(scr[:, 1:2], EPS), w=[EPSC])
        ccol_eps = scr[:, 1:2]

        def norm_h_to_A(gcol):
            rmsnorm(8, hs, lambda c, tt: H[c][tt], gcol, As, lambda c, tt: RA[c][tt], float(D))

        def proj_out(w_dram, panel, WO8, src, src_res, issue=True):
            if issue:
                for oc in range(8):
                    wdma(panel(oc), w_dram[oc], [WO8[oc]], "wo%d" % oc)
            for tt in range(NTT):
                for oc in range(8):
                    wv = panel(oc).rearrange("p (k n) -> p k n", k=8)
                    bank = allb.next()
                    mm_group(bank, [(banks[bank][:, :], wv[:, k, :], src(k, tt)) for k in range(8)],
                             [WO8[oc]] + [src_res(k, tt) for k in range(8)])
                    add("dve", lambda e, bank=bank, oc=oc, tt=tt: e.tensor_tensor(
                        out=hs(oc, tt), in0=banks[bank][:, :], in1=hs(oc, tt), op=ALU.add),
                        r=[PS[bank], H[oc][tt]], w=[H[oc][tt]])

        def mixer0():
            norm_h_to_A(C_ATTN)
            WIN = grid("WIN", 2)
            WO = grid("WO", 8)
            Z = grid("Z", 8, 4)
            MS = grid("MS", 4)
            CS = grid("CSB", 2)
            YS = grid("YS", 2)
            handoff([], sum(Z, []) + MS)
            add("dve", lambda e: e.memset(Dg[:, 0:2], 0.0), w=[MS[0]])

            def win_slot(s_):
                return W2[:, s_ * 3072:(s_ + 1) * 3072]

            def csb(s_):
                return Ef32[:, s_ * TT:(s_ + 1) * TT]

            def ysb(s_):
                return Ef32[:, (2 + s_) * TT:(3 + s_) * TT]

            for j in range(2):
                wdma(win_slot(j), win_d[j], [WIN[j]], "win%d" % j)

            def wo4(oc):
                return W4[:, oc * 1024:(oc + 1) * 1024]
            for oc in range(8):
                wdma(wo4(oc), wout_d[oc], [WO[oc]], "wo%d" % oc)
            it = 0
            for j in range(8):
                s_ = j % 2
                wv = win_slot(s_).rearrange("p (k t n) -> p k t n", k=8, t=3)
                for tt in range(NTT):
                    bb, bc, bu = allb.next(), allb.next(), allb.next()
                    for which, bank in ((0, bb), (1, bc), (2, bu)):
                        mm_group(bank, [(banks[bank][:, :], wv[:, k, which, :], As(k, tt)) for k in range(8)],
                                 [WIN[s_]] + [RA[k][tt] for k in range(8)])
                    t_ = it % 2
                    it += 1
                    add("act", lambda e, t_=t_, bc=bc: e.copy(out=csb(t_), in_=banks[bc][:, :]),
                        r=[PS[bc]], w=[CS[t_]])
                    add("dve", lambda e, t_=t_, bu=bu, tt=tt: e.tensor_tensor(
                        out=Dg[:, 2 + tt * TT: 2 + (tt + 1) * TT], in0=banks[bu][:, :], in1=csb(t_), op=ALU.mult),
                        r=[PS[bu], CS[t_]], w=[MS[tt]])
                    rd = [MS[tt]] + ([MS[tt - 1]] if tt > 0 else [])
                    add("dve", lambda e, t_=t_, tt=tt, j=j: e.tensor_scalar(
                        out=ysb(t_), in0=Dg[:, tt * TT: (tt + 1) * TT], scalar1=ccol(C_SCW + 0 * 8 + j), scalar2=None,
                        op0=ALU.mult), r=rd + [CST], w=[YS[t_]])
                    for tap in (1, 2):
                        add("dve", lambda e, t_=t_, tt=tt, j=j, tap=tap: e.scalar_tensor_tensor(
                            out=ysb(t_), in0=Dg[:, tap + tt * TT: tap + (tt + 1) * TT],
                            scalar=ccol(C_SCW + tap * 8 + j), in1=ysb(t_), op0=ALU.mult, op1=ALU.add),
                            r=rd + [CST, YS[t_]], w=[YS[t_]])
                    add("dve", lambda e, t_=t_, bb=bb, tt=tt, j=j: e.tensor_tensor(
                        out=Bs(j, tt), in0=banks[bb][:, :], in1=ysb(t_), op=ALU.mult),
                        r=[PS[bb], YS[t_]], w=[Z[j][tt]])
                if j + 2 < 8:
                    wdma(win_slot(s_), win_d[j + 2], [WIN[s_]], "win%d" % s_)
            proj_out(wout_d, wo4, WO, Bs, lambda k, tt: Z[k][tt], issue=False)
            return sum(Z, []) + MS + CS + YS + WIN + WO

        def ffn(l, old_res, hook=None):
            norm_h_to_A(C_FFN + l * 8)
            WUP = grid("WUP%d" % l, 3)
            WDN = grid("WDN%d" % l, 2)
            AB = grid("AB%d" % l, 8, 4)
            GS = grid("GS%d" % l, 4)
            YS = grid("FY%d" % l, 2)
            SS = grid("FS%d" % l, 2)
            handoff(old_res, sum(AB, []) + GS + YS + SS + WUP + WDN)
            add("dve", lambda e: e.memset(Dg[:, 0:2], 0.0), w=[GS[0]])
            groups = [(g0, min(g0 + GRP, NHC)) for g0 in range(0, NHC, GRP)]

            def up_slot(s_):
                return W3[:, s_ * 2048:(s_ + 1) * 2048]

            def dn_slot(s_):
                return W4[:, s_ * 4096:(s_ + 1) * 4096]

            def ysb(s_):
                return Ef32[:, s_ * TT:(s_ + 1) * TT]

            def ssb(s_):
                return Ef32[:, (2 + s_) * TT:(3 + s_) * TT]

            def dn_dma(gi):
                g0, g1 = groups[gi]
                s_ = gi % 2
                n = g1 - g0
                wdma(dn_slot(s_)[:, 0:n * 1024].rearrange("p (j n) -> p j n", j=n),
                     wdn_d[l, g0:g1].rearrange("j p n -> p j n"), [WDN[s_]], "wdn%d" % s_)

            for j in range(3):
                wdma(up_slot(j), wup_d[l, j], [WUP[j]], "wup%d" % j)
            dn_dma(0)
            dn_dma(1)
            cnt = [0]

            def up(gi):
                g0, g1 = groups[gi]
                for j in range(g0, g1):
                    s_ = j % 3
                    row = (gi % 2) * 4 + (j - g0)
                    wv = up_slot(s_).rearrange("p (k t n) -> p k t n", k=8, t=2)
                    for tt in range(NTT):
                        bg, bv = allb.next(), allb.next()
                        for which, bank in ((0, bg), (1, bv)):
                            mm_group(bank, [(banks[bank][:, :], wv[:, k, which, :], As(k, tt)) for k in range(8)],
                                     [WUP[s_]] + [RA[k][tt] for k in range(8)])
                        t_ = cnt[0] % 2
                        cnt[0] += 1
                        add("act", lambda e, bg=bg, tt=tt: e.copy(
                            out=Dg[:, 2 + tt * TT: 2 + (tt + 1) * TT], in_=banks[bg][:, :]),
                            r=[PS[bg]], w=[GS[tt]])
                        rd = [GS[tt]] + ([GS[tt - 1]] if tt > 0 else [])
                        cw = C_FCW + l * 66
                        add("dve", lambda e, t_=t_, tt=tt, j=j, cw=cw: e.tensor_scalar(
                            out=ysb(t_), in0=Dg[:, tt * TT:(tt + 1) * TT], scalar1=ccol(cw + j), scalar2=None,
                            op0=ALU.mult), r=rd + [CST], w=[YS[t_]])
                        for tap in (1, 2):
                            add("dve", lambda e, t_=t_, tt=tt, j=j, tap=tap, cw=cw: e.scalar_tensor_tensor(
                                out=ysb(t_), in0=Dg[:, tap + tt * TT: tap + (tt + 1) * TT],
                                scalar=ccol(cw + tap * NHC + j), in1=ysb(t_), op0=ALU.mult, op1=ALU.add),
                                r=rd + [CST, YS[t_]], w=[YS[t_]])
                        add("act", lambda e, t_=t_, j=j: e.activation(
                            out=ssb(t_), in_=ysb(t_), func=AF.Silu, bias=ccol(C_FCB + l * NHC + j)),
                            r=[YS[t_], CST], w=[SS[t_]])
                        add("dve", lambda e, t_=t_, bv=bv, row=row, tt=tt: e.tensor_tensor(
                            out=Bs(row, tt), in0=banks[bv][:, :], in1=ssb(t_), op=ALU.mult),
                            r=[PS[bv], SS[t_]], w=[AB[row][tt]])
                    if j + 3 < NHC:
                        wdma(up_slot(s_), wup_d[l, j + 3], [WUP[s_]], "wup%d" % s_)

            def down(gi):
                g0, g1 = groups[gi]
                s_ = gi % 2
                n = g1 - g0
                wv = dn_slot(s_).rearrange("p (j n) -> p j n", j=4)
                for tt in range(NTT):
                    for oc in range(8):
                        bank = allb.next()
                        mm_group(bank, [(banks[bank][:, :], wv[:, jj, oc * 128:(oc + 1) * 128],
                                         Bs((gi % 2) * 4 + jj, tt)) for jj in range(n)],
                                 [WDN[s_]] + [AB[(gi % 2) * 4 + jj][tt] for jj in range(n)])
                        add("dve", lambda e, bank=bank, oc=oc, tt=tt: e.tensor_tensor(
                            out=hs(oc, tt), in0=banks[bank][:, :], in1=hs(oc, tt), op=ALU.add),
                            r=[PS[bank], H[oc][tt]], w=[H[oc][tt]])
                if gi + 2 < len(groups):
                    dn_dma(gi + 2)

            up(0)
            for gi in range(1, len(groups)):
                up(gi)
                if gi == len(groups) - 1 and hook is not None:
                    hook(dict(WUP=WUP, GS=GS, YS=YS, SS=SS), old_res)
                down(gi - 1)
            down(len(groups) - 1)
            return sum(AB, []) + GS + YS + SS + WUP + WDN

        def rope_args(old_res):
            TB = grid("TB", 4)
            TMP = grid("TMP", 4)
            handoff(old_res, TB + TMP)
            T = Dg[:, 0:S]

            def tmp(i):
                return Ef32[:, i * TT:(i + 1) * TT]
            C1 = 6.28125
            C2 = TWO_PI - C1
            lim = 3.1415925
            for tt in range(NTT):
                sl = slice(tt * TT, (tt + 1) * TT)
                add("sp", lambda e, sl=sl: e.dma_start(out=Ei32[:, 3 * TT:4 * TT], in_=pos_d[:, sl]),
                    w=[TMP[3]], dma="pos")
                add("dve", lambda e: e.tensor_scalar(out=tmp(0), in0=Ei32[:, 3 * TT:4 * TT], scalar1=ccol(C_INVF),
                                                     scalar2=None, op0=ALU.mult), r=[TMP[3], CST], w=[TMP[0]])
                add("dve", lambda e: e.tensor_scalar(out=Ei32[:, 1 * TT:2 * TT], in0=tmp(0), scalar1=1.0 / TWO_PI,
                                                     scalar2=None, op0=ALU.mult), r=[TMP[0]], w=[TMP[1]])
                add("dve", lambda e: e.tensor_copy(out=tmp(2), in_=Ei32[:, 1 * TT:2 * TT]), r=[TMP[1]], w=[TMP[2]])
                add("dve", lambda e: e.scalar_tensor_tensor(out=tmp(0), in0=tmp(2), scalar=-C1, in1=tmp(0),
                                                            op0=ALU.mult, op1=ALU.add), r=[TMP[2], TMP[0]], w=[TMP[0]])
                add("dve", lambda e: e.scalar_tensor_tensor(out=tmp(0), in0=tmp(2), scalar=-C2, in1=tmp(0),
                                                            op0=ALU.mult, op1=ALU.add), r=[TMP[2], TMP[0]], w=[TMP[0]])
                add("dve", lambda e: e.tensor_single_scalar(out=tmp(2), in_=tmp(0), scalar=math.pi, op=ALU.is_gt),
                    r=[TMP[0]], w=[TMP[2]])
                add("dve", lambda e: e.scalar_tensor_tensor(out=tmp(0), in0=tmp(2), scalar=-TWO_PI, in1=tmp(0),
                                                            op0=ALU.mult, op1=ALU.add), r=[TMP[2], TMP[0]], w=[TMP[0]])
                add("dve", lambda e, sl=sl: e.tensor_scalar(out=T[:, sl], in0=tmp(0), scalar1=-lim, scalar2=lim,
                                                            op0=ALU.max, op1=ALU.min), r=[TMP[0]], w=[TB[tt]])
                add("dve", lambda e, sl=sl: e.scalar_tensor_tensor(out=T[0:64, sl], in0=T[0:64, sl], scalar=-1.0,
                                                                   in1=T[0:64, sl], op0=ALU.mult, op1=ALU.min),
                    r=[TB[tt]], w=[TB[tt]])
            return TB, TMP

        def rope_sin(TB):
            for tt in range(NTT):
                sl = slice(tt * TT, (tt + 1) * TT)
                add("act", lambda e, sl=sl: e.activation(out=Dg[:, sl], in_=Dg[:, sl], func=AF.Sin,
                                                         bias=ccol(C_BIAS), scale=ccol(C_SIGN)),
                    r=[TB[tt], CST], w=[TB[tt]])

        early = {}

        def early_hook(loc, mixer_res):
            WKV = grid("WKV", 3)
            WA = grid("WA", 4)
            handoff(loc["WUP"] + [r_ for r_ in mixer_res], WKV + WA)
            wdma(W3[:, 0:2048], wdkv_d[:, :], [WKV[0]], "wkv0")
            wdma(W3[:, 2048:5120], wdq_d[:, :], [WKV[1]], "wkv1")
            wdma(W3[:, 5120:6144], wkr_d[:, :], [WKV[2]], "wkv2")
            wdma(W2[:, 0:2048], wuk_d[:, :], [WA[0]], "wa0")
            wdma(W2[:, 3072:5120], wuv_d[:, :], [WA[1]], "wa1")
            for hh in range(2):
                o = 2048 + hh * 3072
                wdma(W2[:, o:o + 768], wuq_d[hh], [WA[2 + hh]], "wa%d" % (2 + hh))
            TB, TMP = rope_args(loc["GS"] + loc["YS"] + loc["SS"])
            early.update(WKV=WKV, WA=WA, TB=TB, TMP=TMP)

        def kv_phase(old_res):
            TB, TMP, WKV = early["TB"], early["TMP"], early["WKV"]
            CKF = grid("CKF", 3, 4)
            CKV = grid("CKV", 2, 4)
            KR = grid("KR", 4)
            RT = grid("RT", 2)
            handoff(old_res, sum(CKF, []) + sum(CKV, []) + KR + RT)
            norm_h_to_A(C_KVIN)
            wdkv = W3[:, 0:2048].rearrange("p (k n) -> p k n", k=8)
            wdq = W3[:, 2048:5120].rearrange("p (k n) -> p k n", k=8)
            wkr = W3[:, 5120:6144].rearrange("p (k n) -> p k n", k=8)

            def ckf(c, tt):
                return Bf32[:, c * S + tt * TT: c * S + (tt + 1) * TT]

            def ckv(c, tt):
                return W4[:, c * S + tt * TT: c * S + (tt + 1) * TT]

            def kr(tt):
                return W4[0:64, 2 * S + tt * TT: 2 * S + (tt + 1) * TT]

            def krf(tt):
                return W4[:, 2 * S + tt * TT: 2 * S + (tt + 1) * TT]
            add("dve", lambda e: e.memset(W4[64:128, 2 * S:3 * S], 0.0), w=KR)

            for c in range(2):
                for tt in range(NTT):
                    bank = allb.next()
                    mm_group(bank, [(banks[bank][:, :], wdkv[:, k, c * 128:(c + 1) * 128], As(k, tt)) for k in range(8)],
                             [WKV[0]] + [RA[k][tt] for k in range(8)])
                    add("act", lambda e, bank=bank, c=c, tt=tt: e.copy(out=ckf(c, tt), in_=banks[bank][:, :]),
                        r=[PS[bank]], w=[CKF[c][tt]])
            rmsnorm(2, ckf, lambda c, tt: CKF[c][tt], C_KVLAT, ckv, lambda c, tt: CKV[c][tt], 256.0)
            KRAW = grid("KRAW", 4)
            handoff([], KRAW)

            def kraw(tt):
                return Bf32[:, 3 * S + tt * TT: 3 * S + (tt + 1) * TT]
            for tt in range(NTT):
                bank = allb.next()
                mm_group(bank, [(banks[bank][:, :], wkr[:, k, :], As(k, tt)) for k in range(8)],
                         [WKV[2]] + [RA[k][tt] for k in range(8)])
                add("act", lambda e, bank=bank, tt=tt: e.copy(out=kraw(tt), in_=banks[bank][:, :]),
                    r=[PS[bank]], w=[KRAW[tt]])

            def krope_finish():
                rope_sin(TB)
                for tt in range(NTT):
                    i = rope_cnt[0] % 2
                    rope_cnt[0] += 1
                    tb = E[:, 9728 + i * TT: 9728 + (i + 1) * TT]
                    add("dve", lambda e, tt=tt, tb=tb: e.tensor_tensor(
                        out=tb, in0=kraw(tt), in1=Dg[:, tt * TT:(tt + 1) * TT], op=ALU.mult),
                        r=[KRAW[tt], TB[tt]], w=[RT[i]])
                    b2 = allb.next()
                    add("pe", lambda e, b2=b2, tb=tb: e.matmul(banks[b2][:, :], Jm, tb, start=True, stop=True),
                        r=[RT[i], CST], w=[PS[b2]])
                    add("dve", lambda e, b2=b2, tt=tt: e.tensor_copy(out=kr(tt), in_=banks[b2][0:64, :]),
                        r=[PS[b2]], w=[KR[tt]])
            return dict(TB=TB, TMP=TMP, WKV=WKV, CKF=CKF, CKV=CKV, KR=KR, RT=RT, ckv=ckv, kr=krf, wdq=wdq,
                        KRAW=KRAW, krope_finish=krope_finish)

        rope_cnt = [0]

        def rope_fin(bank, tt, TB, RT, dst_ap, dst_res):
            i = rope_cnt[0] % 2
            rope_cnt[0] += 1
            tb = E[:, 9728 + i * TT: 9728 + (i + 1) * TT]
            add("dve", lambda e: e.tensor_tensor(out=tb, in0=banks[bank][:, :], in1=Dg[:, tt * TT:(tt + 1) * TT],
                                                 op=ALU.mult), r=[PS[bank], TB[tt]], w=[RT[i]])
            b2 = allb.next()
            add("pe", lambda e: e.matmul(banks[b2][:, :], Jm, tb, start=True, stop=True),
                r=[RT[i], CST], w=[PS[b2]])
            add("act", lambda e: e.copy(out=dst_ap, in_=banks[b2][0:64, :]), r=[PS[b2]], w=[dst_res])

        def mla(kv):
            TB, RT, CKV, KR, WKV, CKF = kv["TB"], kv["RT"], kv["CKV"], kv["KR"], kv["WKV"], kv["CKF"]
            ckv, kr, wdq = kv["ckv"], kv["kr"], kv["wdq"]
            norm_h_to_A(C_ATTN + 8)
            CQ = grid("CQ", 3, 4)
            PT = grid("PT", 7)
            handoff(kv["TMP"], sum(CQ, []) + PT)

            def cqf(c, tt):
                return Bf32[:, c * S + tt * TT: c * S + (tt + 1) * TT]

            def cq(c, tt):
                return E[:, c * S + tt * TT: c * S + (tt + 1) * TT]

            def pt(i):
                return E[:, 6144 + i * TT: 6144 + (i + 1) * TT]

            for c in range(3):
                for tt in range(NTT):
                    bank = allb.next()
                    mm_group(bank, [(banks[bank][:, :], wdq[:, k, c * 128:(c + 1) * 128], As(k, tt)) for k in range(8)],
                             [WKV[1]] + [RA[k][tt] for k in range(8)])
                    add("act", lambda e, bank=bank, c=c, tt=tt: e.copy(out=cqf(c, tt), in_=banks[bank][:, :]),
                        r=[PS[bank]], w=[CKF[c][tt]])
            rmsnorm(3, cqf, lambda c, tt: CKF[c][tt], C_QLAT, cq, lambda c, tt: CQ[c][tt], 384.0)
            kv["krope_finish"]()
            WO = grid("WO1", 8)
            handoff(WKV, WO)

            def wo31(oc):
                return W3[:, oc * 1024:(oc + 1) * 1024] if oc < 6 else W1[:, (oc - 6) * 1024:(oc - 5) * 1024]
            for oc in range(8):
                wdma(wo31(oc), wo_d[oc], [WO[oc]], "wo%d" % oc)

            HB = grid("HB", 2, 4, 4)
            WA = early["WA"]
            OT = RA
            handoff(sum(CKF, []) + kv["KRAW"], sum(sum(HB, []), []))
            wuk = W2[:, 0:2048].rearrange("p (k n) -> p k n", k=2)
            wuv = W2[:, 3072:5120].rearrange("p (k n) -> p k n", k=2)

            def wuq_slot(s_):
                o = 2048 + s_ * 3072
                return W2[:, o:o + 768]
            for s2 in range(2):
                add("dve", lambda e, s2=s2: e.memset(B[64:128, (4 * s2 + 2) * S:(4 * s2 + 3) * S], 0.0), w=HB[s2][2])
            for d in range(4):
                add("dve", lambda e, d=d: e.memset(pt(3 + d)[64:128, 128 * d:128 * d + 64], 0.0), w=[PT[3 + d]])

            sb_ = BankPool([0, 1, 2])
            ob_ = BankPool([3, 4])
            lb_ = BankPool([5, 6])
            mb_ = BankPool([7])
            pt_cnt = [0]

            def prep_pieces(hd):
                s_ = hd % 2
                wq = wuq_slot(s_).rearrange("p (k n) -> p k n", k=3)
                pieces = []

                def p_qn(tt):
                    bank = mb_.next()
                    mm_group(bank, [(banks[bank][:, :], wq[:, k, 0:128], cq(k, tt)) for k in range(3)],
                             [WA[2 + s_]] + [CQ[k][tt] for k in range(3)])
                    add("dve", lambda e: e.tensor_copy(out=Bs(4 * s_ + 0, tt), in_=banks[bank][:, :]),
                        r=[PS[bank]], w=[HB[s_][0][tt]])

                def p_k(tt):
                    bank = mb_.next()
                    mm_group(bank, [(banks[bank][:, :], wuk[:, k, hd * 128:(hd + 1) * 128], ckv(k, tt)) for k in range(2)],
                             [WA[0]] + [CKV[k][tt] for k in range(2)])
                    add("dve", lambda e: e.tensor_copy(out=Bs(4 * s_ + 1, tt), in_=banks[bank][:, :]),
                        r=[PS[bank]], w=[HB[s_][1][tt]])

                qr_tb = {}

                def p_qr(tt):
                    bank = mb_.next()
                    mm_group(bank, [(banks[bank][:, :], wq[:, k, 128:256], cq(k, tt)) for k in range(3)],
                             [WA[2 + s_]] + [CQ[k][tt] for k in range(3)])
                    i = rope_cnt[0] % 2
                    rope_cnt[0] += 1
                    tb = E[:, 9728 + i * TT: 9728 + (i + 1) * TT]
                    qr_tb[tt] = (i, tb)
                    add("dve", lambda e: e.tensor_tensor(out=tb, in0=banks[bank][:, :], in1=Dg[:, tt * TT:(tt + 1) * TT],
                                                         op=ALU.mult), r=[PS[bank], TB[tt]], w=[RT[i]])

                def p_qr2(tt):
                    i, tb = qr_tb[tt]
                    b2 = mb_.next()
                    add("pe", lambda e: e.matmul(banks[b2][:, :], Jm, tb, start=True, stop=True),
                        r=[RT[i], CST], w=[PS[b2]])
                    add("dve", lambda e: e.tensor_copy(out=Bs(4 * s_ + 2, tt)[0:64, :], in_=banks[b2][0:64, :]),
                        r=[PS[b2]], w=[HB[s_][2][tt]])

                def p_v(tt):
                    bank = mb_.next()
                    pairs = []
                    for kb4 in range(4):
                        for k in range(2):
                            pairs.append((banks[bank][:, kb4 * 128:(kb4 + 1) * 128],
                                          ckv(k, tt)[:, kb4 * 128:(kb4 + 1) * 128],
                                          wuv[:, k, hd * 128:(hd + 1) * 128], k))

                    def fn(e):
                        ins = None
                        for (o, l, r, k) in pairs:
                            ins = e.matmul(o, l, r, start=(k == 0), stop=(k == 1))
                        return ins
                    add("pe", fn, r=[WA[1]] + [CKV[k][tt] for k in range(2)], w=[PS[bank]])
                    add("dve", lambda e: e.tensor_copy(out=Bs(4 * s_ + 3, tt), in_=banks[bank][:, :]),
                        r=[PS[bank]], w=[HB[s_][3][tt]])

                for tt in range(NTT):
                    for f in (p_qr, p_qn, p_k, p_qr2, p_v):
                        pieces.append(lambda f=f, tt=tt: f(tt))

                def p_dma():
                    if hd + 2 < NHEAD:
                        wdma(wuq_slot(s_), wuq_d[hd + 2], [WA[2 + s_]], "wa%d" % (2 + s_))
                pieces.append(p_dma)
                return pieces

            def bufs(hd):
                s_ = hd % 2
                qn = lambda c0, c1: B[:, (4 * s_ + 0) * S + c0:(4 * s_ + 0) * S + c1]
                Kh = lambda c0, c1: B[:, (4 * s_ + 1) * S + c0:(4 * s_ + 1) * S + c1]
                qr = lambda c0, c1: B[:, (4 * s_ + 2) * S + c0:(4 * s_ + 2) * S + c1]
                Vh = lambda kb: B[:, (4 * s_ + 3) * S + kb * 128:(4 * s_ + 3) * S + (kb + 1) * 128]
                return qn, Kh, qr, Vh

            def emit_S(hd, qt, kb):
                s_ = hd % 2
                qn, Kh, qr, Vh = bufs(hd)
                q0 = qt * TT
                d = kb - 4 * qt
                c0 = 128 * d if d > 0 else 0
                sbk = sb_.next()
                ktt = kb // 4
                mm_group(sbk, [(banks[sbk][:, c0:TT], Kh(kb * 128, (kb + 1) * 128), qn(q0 + c0, q0 + TT)),
                               (banks[sbk][:, c0:TT], kr(ktt)[:, (kb % 4) * 128:(kb % 4 + 1) * 128],
                                qr(q0 + c0, q0 + TT))],
                         [HB[s_][1][ktt], HB[s_][0][qt], KR[ktt], HB[s_][2][qt]])
                return sbk

            cur = {}

            def emit_rest(hd, qt, kb, sbk):
                s_ = hd % 2
                qn, Kh, qr, Vh = bufs(hd)
                nkb = 4 * qt + 4
                d = kb - 4 * qt
                c0 = 128 * d if d > 0 else 0
                ktt = kb // 4
                if kb == 0:
                    cur["ob"], cur["lb"] = ob_.next(), lb_.next()
                ob, lb = cur["ob"], cur["lb"]
                if d < 0:
                    pi_ = pt_cnt[0] % 3
                    pt_cnt[0] += 1
                    add("act", lambda e: e.activation(
                        out=pt(pi_), in_=banks[sbk][:, :], func=AF.Exp, scale=SCALE),
                        r=[PS[sbk]], w=[PT[pi_]])
                else:
                    pi_ = 3 + d
                    add("act", lambda e: e.activation(
                        out=pt(pi_)[:, c0 + 64:TT], in_=banks[sbk][:, c0 + 64:TT], func=AF.Exp, scale=SCALE),
                        r=[PS[sbk]], w=[PT[pi_]])
                    add("act", lambda e: e.activation(
                        out=pt(pi_)[0:64, c0:c0 + 64], in_=banks[sbk][0:64, c0:c0 + 64], func=AF.Exp, scale=SCALE),
                        r=[PS[sbk]], w=[PT[pi_]])
                first, last = (kb == 0), (kb == nkb - 1)
                add("pe", lambda e: e.matmul(
                    banks[ob][:, c0:TT], Vh(kb), pt(pi_)[:, c0:TT], start=first, stop=last),
                    r=[HB[s_][3][ktt], PT[pi_]], w=[PS[ob]])
                add("pe", lambda e: e.matmul(
                    banks[lb][:, c0:TT], ones, pt(pi_)[:, c0:TT], start=first, stop=last),
                    r=[CST, PT[pi_]], w=[PS[lb]])
                if last:
                    for ch in range(4):
                        todo.append(lambda ch=ch: add("dve", lambda e: e.reciprocal(
                            out=rs2[:, ch * 128:(ch + 1) * 128], in_=banks[lb][:, ch * 128:(ch + 1) * 128]),
                            r=[PS[lb]], w=SQ))
                    todo.append(lambda: add("dve", lambda e: e.tensor_tensor(
                        out=As(hd, qt), in0=banks[ob][:, :], in1=rs2[:, :], op=ALU.mult),
                        r=[PS[ob]] + SQ, w=[OT[hd][qt]]))

            todo = []
            for p in prep_pieces(0):
                p()
            steps = [(hd, qt, kb) for hd in range(NHEAD) for qt in range(NTT) for kb in range(4 * qt + 4)]
            LOOK = 2
            pieces = []
            sq_ = []
            for j in range(LOOK):
                sq_.append(emit_S(*steps[j]))
            for i, (hd, qt, kb) in enumerate(steps):
                if qt == 0 and kb == 0:
                    for p in pieces:
                        p()
                    pieces = prep_pieces(hd + 1) if hd + 1 < NHEAD else []
                sbk = sq_.pop(0)
                if i + LOOK < len(steps):
                    if steps[i + LOOK][0] != hd:
                        for p in pieces:
                            p()
                        pieces = []
                    sq_.append(emit_S(*steps[i + LOOK]))
                if pieces and i % 2 == 1:
                    pieces.pop(0)()
                emit_rest(hd, qt, kb, sbk)
                if todo:
                    todo.pop(0)()
            for t_ in todo:
                t_()

            proj_out(wo_d, wo31, WO, As, lambda k, tt: OT[k][tt], issue=False)
            return (sum(CQ, []) + PT + sum(sum(HB, []), []) + WA + WO + TB + RT + sum(CKV, []) + KR + WKV)

        def final(do_norm):
            if do_norm:
                rmsnorm(8, hs, lambda c, tt: H[c][tt], C_FINAL, hs, lambda c, tt: H[c][tt], float(D))
            for tt in range(NTT):
                add("sp", lambda e, tt=tt: e.dma_start(out=o3[:, :, tt * TT:(tt + 1) * TT], in_=h3[:, :, tt * TT:(tt + 1) * TT]),
                    r=[H[c][tt] for c in range(8)], w=[OUT[tt]], dma="o%d" % tt)
            add("sp", None, r=OUT, w=[])

        stages = ["mixer0", "ffn0", "kv", "mla", "ffn1"]
        n_st = len(stages) if stop_after is None else stages.index(stop_after) + 1
        res = []
        kv = None
        if n_st >= 1:
            res = mixer0()
        if n_st >= 2:
            res = ffn(0, res, hook=early_hook)
        if n_st >= 3:
            kv = kv_phase(res)
        if n_st >= 4:
            res = mla(kv)
        if n_st >= 5:
            res = ffn(1, res)
        final(final_norm and stop_after is None)

        with nc.Block() as block:
            prog.emit(nc, block, stack)
    nc._prog_stats = prog.stats
    return nc


def _kchunk(w):
    K, N = w.shape
    return np.ascontiguousarray(w.reshape(K // 128, 128, N).transpose(1, 0, 2))


def _prep_shared(inp):
    f = np.float32
    sh = {}
    cst = np.zeros((128, C_NCOL), f)

    def put(col, vec):
        n = vec.shape[0] // 128
        cst[:, col:col + n] = vec.reshape(n, 128).T
    for l in range(2):
        put(C_ATTN + l * 8, inp["attn_norm"][l])
        put(C_FFN + l * 8, inp["ffn_norm"][l])
    put(C_FINAL, inp["final_norm"])
    put(C_KVIN, inp["kv_in_norm"])
    put(C_KVLAT, inp["kv_latent_norm"])
    put(C_QLAT, inp["q_latent_norm"][0])
    for tap in range(3):
        put(C_SCW + tap * 8, inp["sc_conv_w"][0, tap])
        for l in range(2):
            put(C_FCW + l * 66 + tap * NHC, inp["ffn_conv_w"][l, tap])
    for l in range(2):
        put(C_FCB + l * NHC, inp["ffn_conv_b"][l])
    half = 32
    inv_freq = (1.0 / (np.float32(10000.0) ** (np.arange(half, dtype=f) / np.float32(half)))).astype(f)
    cst[:, C_INVF] = inv_freq[np.arange(128) % 32]
    sign = np.ones(128, f)
    sign[64:96] = -1.0
    cst[:, C_SIGN] = sign
    cst[0:64, C_BIAS] = np.float32(math.pi / 2)
    sh["cst"] = cst
    cb = np.zeros((128, 256), f)
    cb[:, 0:128] = 1.0
    pp = np.arange(128)
    cb[:, 128:256] = (pp[:, None] % 64 == pp[None, :] % 64).astype(f)
    sh["cb"] = cb
    w_in = _kchunk(inp["sc_w_in"][0])
    w_in = w_in.reshape(128, 8, 3, 8, 128).transpose(3, 0, 1, 2, 4)
    sh["w_in"] = np.ascontiguousarray(w_in).reshape(8, 128, 8 * 3 * 128)

    def panels(w):
        t = _kchunk(w).reshape(128, 8, 8, 128).transpose(2, 0, 1, 3)
        return np.ascontiguousarray(t).reshape(8, 128, 8 * 128)
    sh["w_out"] = panels(inp["sc_w_out"][0])
    sh["w_o"] = panels(inp["w_o"][0])
    wup = np.stack([_kchunk(inp["ffn_w_up"][l]) for l in range(2)])
    wup = wup.reshape(2, 128, 8, 2, NHC, 128).transpose(0, 4, 1, 2, 3, 5)
    sh["w_up"] = np.ascontiguousarray(wup).reshape(2, NHC, 128, 8 * 2 * 128)
    sh["w_down"] = np.ascontiguousarray(inp["ffn_w_down"].reshape(2, NHC, 128, 1024))
    sh["w_dkv"] = _kchunk(inp["w_dkv"]).reshape(128, 8 * 256)
    wkr = inp["w_kr"]
    wkr2 = np.concatenate([wkr, wkr[:, 32:64], wkr[:, 0:32]], axis=1)
    sh["w_kr"] = _kchunk(wkr2).reshape(128, 8 * 128)
    sh["w_uk"] = _kchunk(inp["w_uk"]).reshape(128, 2 * 1024)
    sh["w_uv"] = _kchunk(inp["w_uv"]).reshape(128, 2 * 1024)
    sh["w_dq"] = _kchunk(inp["w_dq"][0]).reshape(128, 8 * 384)
    wuq = _kchunk(inp["w_uq"][0]).reshape(128, 3, 8, 192)
    wuq2 = np.concatenate([wuq, wuq[..., 160:192], wuq[..., 128:160]], axis=-1)
    sh["w_uq"] = np.ascontiguousarray(wuq2.transpose(2, 0, 1, 3)).reshape(8, 128, 3 * 256)
    return sh


def _prep_core(inp, b):
    x = inp["x"][b]
    xT = np.ascontiguousarray(x.T.reshape(8, 128, S).transpose(1, 0, 2)).reshape(128, 8 * S)
    pos = np.ascontiguousarray(np.broadcast_to(inp["positions"][b].astype(np.int32)[None, :], (128, S)))
    return {"xT": xT, "pos": pos}


_NC_CACHE = {}


def run(inputs, stop_after=None, final_norm=True, trace=False):
    inp = {k: np.asarray(v) for k, v in inputs.items()}
    key = (stop_after, final_norm)
    if key not in _NC_CACHE:
        _NC_CACHE[key] = build(stop_after, final_norm)
    nc = _NC_CACHE[key]
    sh = _prep_shared(inp)
    in_maps = []
    for b in range(8):
        m = dict(sh)
        m.update(_prep_core(inp, b))
        in_maps.append(m)
    res = run_bass_kernel_spmd(nc, in_maps, core_ids=list(range(8)), trace=trace)
    outs = []
    for b in range(8):
        oT = np.asarray(res.results[b]["outT"]).reshape(128, 8, S)
        outs.append(oT.transpose(2, 1, 0).reshape(S, D))
    return np.stack(outs).astype(np.float32), res


def kernel(**inputs):
    out, _ = run(inputs)
    return out
```

```python
import contextlib
import math
import numpy as np
import concourse.bass as bass
import concourse.mybir as mybir
from concourse.bass_utils import run_bass_kernel_spmd

F32 = mybir.dt.float32
BF16 = mybir.dt.bfloat16
I32 = mybir.dt.int32
ALU = mybir.AluOpType
AF = mybir.ActivationFunctionType

S = 2048
D = 1024
NTT = 4
TT = 512
DFF = 2816
NHC = 22
GRP = 4
NHEAD = 8
EPS = 1e-6
SCALE = 192.0 ** -0.5
TWO_PI = 2.0 * math.pi

C_ATTN = 0
C_FFN = 16
C_FINAL = 32
C_KVIN = 40
C_KVLAT = 48
C_QLAT = 50
C_SCW = 56
C_FCW = 80
C_FCB = 212
C_INVF = 256
C_SIGN = 257
C_BIAS = 258
C_NCOL = 264

SAME_ENG_SYNC = True


class Res:
    __slots__ = ("name", "last_w", "readers")

    def __init__(self, name):
        self.name = name
        self.last_w = None
        self.readers = []


class Op:
    __slots__ = ("eng", "fn", "deps", "dma", "signal", "seq", "waits", "idx")


class Prog:
    ENGS = ("pe", "act", "dve", "pool", "sp")

    def __init__(self):
        self.ops = []

    def add(self, eng, fn, r=(), w=(), dma=None):
        op = Op()
        op.eng = eng
        op.fn = fn
        op.dma = dma
        op.idx = len(self.ops)
        op.signal = False
        op.seq = None
        deps = {}
        for res in r:
            if res.last_w is not None:
                deps[res.last_w] = True
        for res in w:
            if res.last_w is not None:
                deps[res.last_w] = True
            for rd in res.readers:
                deps.setdefault(rd, False)
        deps.pop(op.idx, None)
        op.deps = deps
        for res in r:
            res.readers.append(op.idx)
        for res in w:
            res.last_w = op.idx
            res.readers = []
        self.ops.append(op)
        return op

    def emit(self, nc, block, stack):
        ops = self.ops
        dma_last = {}
        dma_cnt = {}
        for op in ops:
            if op.dma is not None:
                prev = dma_last.get(op.dma)
                if prev is not None:
                    op.deps[prev] = True
                dma_last[op.dma] = op.idx
                dma_cnt[op.dma] = dma_cnt.get(op.dma, 0) + 16
                op.seq = dma_cnt[op.dma]
        need = []
        for op in ops:
            lst = []
            for d, strong in op.deps.items():
                y = ops[d]
                if y.dma is not None:
                    lst.append(d)
                    continue
                if y.eng == op.eng and op.dma is None:
                    if y.eng == "pe":
                        continue
                    if not SAME_ENG_SYNC:
                        continue
                y.signal = True
                lst.append(d)
            need.append(lst)
        cnt = {e: 0 for e in self.ENGS}
        for op in ops:
            if op.dma is None and op.signal:
                cnt[op.eng] += 1
                op.seq = cnt[op.eng]
        esem = {e: stack.enter_context(nc.semaphore("s_" + e)) for e in self.ENGS}
        dsem = {k: stack.enter_context(nc.semaphore("d_%s" % (k,))) for k in dma_cnt}
        waited = {e: {} for e in self.ENGS}
        per_eng = {e: [] for e in self.ENGS}
        for op, lst in zip(ops, need):
            ws = {}
            for d in lst:
                y = ops[d]
                key = ("d", y.dma) if y.dma is not None else ("e", y.eng)
                ws[key] = max(ws.get(key, 0), y.seq)
            out = []
            for key, val in ws.items():
                if waited[op.eng].get(key, 0) >= val:
                    continue
                waited[op.eng][key] = val
                out.append((dsem[key[1]] if key[0] == "d" else esem[key[1]], val))
            op.waits = out
            per_eng[op.eng].append(op)

        def run(eng_name):
            def body(e):
                for op in per_eng[eng_name]:
                    for sem, val in op.waits:
                        e.wait_ge(sem, val)
                    if op.fn is None:
                        continue
                    ins = op.fn(e)
                    if op.dma is not None:
                        ins.then_inc(dsem[op.dma], 16)
                    elif op.signal:
                        ins.then_inc(esem[eng_name], 1)
            return body

        block.tensor(run("pe"))
        block.scalar(run("act"))
        block.vector(run("dve"))
        block.gpsimd(run("pool"))
        block.sync(run("sp"))
        self.stats = {e: len(per_eng[e]) for e in self.ENGS}


def grid(name, *dims):
    if len(dims) == 1:
        return [Res("%s%d" % (name, i)) for i in range(dims[0])]
    return [grid("%s%d_" % (name, i), *dims[1:]) for i in range(dims[0])]


def build(stop_after=None, final_norm=True):
    nc = bass.Bass("TRN2", target_bir_lowering=False)
    prog = Prog()
    add = prog.add

    def dram(name, shape, dt, kind="ExternalInput"):
        return nc.dram_tensor(name, shape, dt, kind=kind).ap()

    xT_d = dram("xT", [128, 8 * S], F32)
    pos_d = dram("pos", [128, S], I32)
    cst_d = dram("cst", [128, C_NCOL], F32)
    cb_d = dram("cb", [128, 256], F32)
    win_d = dram("w_in", [8, 128, 8 * 3 * 128], F32)
    wout_d = dram("w_out", [8, 128, 8 * 128], F32)
    wup_d = dram("w_up", [2, NHC, 128, 8 * 2 * 128], F32)
    wdn_d = dram("w_down", [2, NHC, 128, 1024], F32)
    wdkv_d = dram("w_dkv", [128, 8 * 256], F32)
    wkr_d = dram("w_kr", [128, 8 * 128], F32)
    wuk_d = dram("w_uk", [128, 2 * 1024], F32)
    wuv_d = dram("w_uv", [128, 2 * 1024], F32)
    wdq_d = dram("w_dq", [128, 8 * 384], F32)
    wuq_d = dram("w_uq", [8, 128, 3 * 256], F32)
    wo_d = dram("w_o", [8, 128, 8 * 128], F32)
    out_d = dram("outT", [128, 8 * S], F32, kind="ExternalOutput")

    with contextlib.ExitStack() as stack:
        def sb(name, shape, dt):
            return stack.enter_context(nc.sbuf_tensor("sb_" + name, shape, dt))

        h = sb("h", [128, 8 * S], F32)
        A = sb("A", [128, 8 * S], BF16)
        B = sb("B", [128, 8 * S], BF16)
        W1 = sb("W1", [128, 2 * 1024], BF16)
        W2 = sb("W2", [128, 6144], BF16)
        W3 = sb("W3", [128, 6144], BF16)
        W4 = sb("W4", [128, 8192], BF16)
        Dg = sb("Dg", [128, 2050], F32)
        E = sb("E", [128, 10752], BF16)
        cst = sb("cst", [128, C_NCOL], F32)
        cb = sb("cb", [128, 256], BF16)
        sq = sb("sq", [128, 2 * TT], BF16)
        rs = sb("rs", [128, TT], F32)
        scr = sb("scr", [128, 8], F32)
        banks = [stack.enter_context(nc.psum_tensor("ps%d" % i, [128, TT], F32)) for i in range(8)]

        Ef32 = E[:, :].bitcast(F32)
        Ei32 = E[:, :].bitcast(I32)
        Bf32 = B[:, :].bitcast(F32)

        rs2 = sq[:, :].bitcast(F32)
        ones = cb[:, 0:128]
        Jm = cb[:, 128:256]

        H = grid("H", 8, 4)
        RA = grid("A", 8, 4)
        PS = grid("PS", 8)
        SQ = grid("SQ", 2)
        RS = Res("RS")
        CST = Res("CST")
        SCR = Res("SCR")
        OUT = grid("OUT", 8)

        class BankPool:
            def __init__(self, ids):
                self.ids = list(ids)
                self.i = 0

            def next(self):
                b = self.ids[self.i % len(self.ids)]
                self.i += 1
                return b

        allb = BankPool(range(8))

        def hs(c, tt):
            return h[:, c * S + tt * TT: c * S + (tt + 1) * TT]

        def As(c, tt):
            return A[:, c * S + tt * TT: c * S + (tt + 1) * TT]

        def Bs(r, tt):
            return B[:, r * S + tt * TT: r * S + (tt + 1) * TT]

        def ccol(i):
            return cst[:, i:i + 1]

        def handoff(old, new):
            add("dve", lambda e: e.memset(scr[:, 0:1], 0.0), r=(), w=list(old) + list(new) + [SCR])

        def wdma(out_ap, in_ap, wres, key):
            add("pool", lambda e: e.dma_start(out=out_ap, in_=in_ap, max_dma_last_dim=4096), r=(), w=wres, dma=key)

        def mm_group(bank, pairs, rres):
            n = len(pairs)

            def fn(e):
                ins = None
                for i, (o, l, r) in enumerate(pairs):
                    ins = e.matmul(o, l, r, start=(i == 0), stop=(i == n - 1))
                return ins
            add("pe", fn, r=rres, w=[PS[bank]])

        add("sp", lambda e: e.dma_start(out=cst[:, :], in_=cst_d[:, :]), w=[CST], dma="cst")
        wdma(cb[:, :], cb_d[:, :], [CST], "cb")
        h3 = h[:, :].rearrange("p (c t) -> p c t", c=8)
        x3 = xT_d[:, :].rearrange("p (c t) -> p c t", c=8)
        o3 = out_d[:, :].rearrange("p (c t) -> p c t", c=8)
        for tt in range(NTT):
            add("sp", lambda e, tt=tt: e.dma_start(out=h3[:, :, tt * TT:(tt + 1) * TT], in_=x3[:, :, tt * TT:(tt + 1) * TT]),
                w=[H[c][tt] for c in range(8)], dma="x%d" % tt)

        def rmsnorm(nch, src, src_res, gcol, dst, dst_res, nfeat):
            for tt in range(NTT):
                bank = allb.next()
                for c in range(nch):
                    s_ = c % 2
                    add("act", lambda e, c=c, s_=s_, tt=tt: e.activation(
                        out=sq[:, s_ * TT:(s_ + 1) * TT], in_=src(c, tt), func=AF.Square),
                        r=[src_res(c, tt)], w=[SQ[s_]])
                    add("pe", lambda e, c=c, s_=s_, bank=bank: e.matmul(
                        banks[bank][:, :], ones, sq[:, s_ * TT:(s_ + 1) * TT],
                        start=(c == 0), stop=(c == nch - 1)),
                        r=[SQ[s_], CST], w=[PS[bank]])
                add("act", lambda e, bank=bank: e.activation(
                    out=rs[:, :], in_=banks[bank][:, :], func=AF.Ln, bias=ccol_eps, scale=1.0 / nfeat),
                    r=[PS[bank], EPSC], w=[RS])
                add("act", lambda e: e.activation(out=rs[:, :], in_=rs[:, :], func=AF.Exp, scale=-0.5),
                    r=[RS], w=[RS])
                for c in range(nch):
                    add("dve", lambda e, c=c, tt=tt: e.scalar_tensor_tensor(
                        out=dst(c, tt), in0=src(c, tt), scalar=ccol(gcol + c), in1=rs[:, :],
                        op0=ALU.mult, op1=ALU.mult),
                        r=[src_res(c, tt), RS, CST], w=[dst_res(c, tt)])

        EPSC = Res("EPSC")
        add("dve", lambda e: e.memset(scr[:, 1:2], EPS), w=[EPSC])
        ccol_eps = scr[:, 1:2]

        def norm_h_to_A(gcol):
            rmsnorm(8, hs, lambda c, tt: H[c][tt], gcol, As, lambda c, tt: RA[c][tt], float(D))

        def proj_out(w_dram, panel, WO8, src, src_res, issue=True):
            if issue:
                for oc in range(8):
                    wdma(panel(oc), w_dram[oc], [WO8[oc]], "wo%d" % oc)
            for tt in range(NTT):
                for oc in range(8):
                    wv = panel(oc).rearrange("p (k n) -> p k n", k=8)
                    bank = allb.next()
                    mm_group(bank, [(banks[bank][:, :], wv[:, k, :], src(k, tt)) for k in range(8)],
                             [WO8[oc]] + [src_res(k, tt) for k in range(8)])
                    add("dve", lambda e, bank=bank, oc=oc, tt=tt: e.tensor_tensor(
                        out=hs(oc, tt), in0=banks[bank][:, :], in1=hs(oc, tt), op=ALU.add),
                        r=[PS[bank], H[oc][tt]], w=[H[oc][tt]])

        def mixer0():
            norm_h_to_A(C_ATTN)
            WIN = grid("WIN", 2)
            WO = grid("WO", 8)
            Z = grid("Z", 8, 4)
            MS = grid("MS", 4)
            CS = grid("CSB", 2)
            YS = grid("YS", 2)
            handoff([], sum(Z, []) + MS)
            add("dve", lambda e: e.memset(Dg[:, 0:2], 0.0), w=[MS[0]])

            def win_slot(s_):
                return W2[:, s_ * 3072:(s_ + 1) * 3072]

            def csb(s_):
                return Ef32[:, s_ * TT:(s_ + 1) * TT]

            def ysb(s_):
                return Ef32[:, (2 + s_) * TT:(3 + s_) * TT]

            for j in range(2):
                wdma(win_slot(j), win_d[j], [WIN[j]], "win%d" % j)

            def wo4(oc):
                return W4[:, oc * 1024:(oc + 1) * 1024]
            for oc in range(8):
                wdma(wo4(oc), wout_d[oc], [WO[oc]], "wo%d" % oc)
            it = 0
            for j in range(8):
                s_ = j % 2
                wv = win_slot(s_).rearrange("p (k t n) -> p k t n", k=8, t=3)
                for tt in range(NTT):
                    bb, bc, bu = allb.next(), allb.next(), allb.next()
                    for which, bank in ((0, bb), (1, bc), (2, bu)):
                        mm_group(bank, [(banks[bank][:, :], wv[:, k, which, :], As(k, tt)) for k in range(8)],
                                 [WIN[s_]] + [RA[k][tt] for k in range(8)])
                    t_ = it % 2
                    it += 1
                    add("act", lambda e, t_=t_, bc=bc: e.copy(out=csb(t_), in_=banks[bc][:, :]),
                        r=[PS[bc]], w=[CS[t_]])
                    add("dve", lambda e, t_=t_, bu=bu, tt=tt: e.tensor_tensor(
                        out=Dg[:, 2 + tt * TT: 2 + (tt + 1) * TT], in0=banks[bu][:, :], in1=csb(t_), op=ALU.mult),
                        r=[PS[bu], CS[t_]], w=[MS[tt]])
                    rd = [MS[tt]] + ([MS[tt - 1]] if tt > 0 else [])
                    add("dve", lambda e, t_=t_, tt=tt, j=j: e.tensor_scalar(
                        out=ysb(t_), in0=Dg[:, tt * TT: (tt + 1) * TT], scalar1=ccol(C_SCW + 0 * 8 + j), scalar2=None,
                        op0=ALU.mult), r=rd + [CST], w=[YS[t_]])
                    for tap in (1, 2):
                        add("dve", lambda e, t_=t_, tt=tt, j=j, tap=tap: e.scalar_tensor_tensor(
                            out=ysb(t_), in0=Dg[:, tap + tt * TT: tap + (tt + 1) * TT],
                            scalar=ccol(C_SCW + tap * 8 + j), in1=ysb(t_), op0=ALU.mult, op1=ALU.add),
                            r=rd + [CST, YS[t_]], w=[YS[t_]])
                    add("dve", lambda e, t_=t_, bb=bb, tt=tt, j=j: e.tensor_tensor(
                        out=Bs(j, tt), in0=banks[bb][:, :], in1=ysb(t_), op=ALU.mult),
                        r=[PS[bb], YS[t_]], w=[Z[j][tt]])
                if j + 2 < 8:
                    wdma(win_slot(s_), win_d[j + 2], [WIN[s_]], "win%d" % s_)
            proj_out(wout_d, wo4, WO, Bs, lambda k, tt: Z[k][tt], issue=False)
            return sum(Z, []) + MS + CS + YS + WIN + WO

        def ffn(l, old_res, hook=None):
            norm_h_to_A(C_FFN + l * 8)
            WUP = grid("WUP%d" % l, 3)
            WDN = grid("WDN%d" % l, 2)
            AB = grid("AB%d" % l, 8, 4)
            GS = grid("GS%d" % l, 4)
            YS = grid("FY%d" % l, 2)
            SS = grid("FS%d" % l, 2)
            handoff(old_res, sum(AB, []) + GS + YS + SS + WUP + WDN)
            add("dve", lambda e: e.memset(Dg[:, 0:2], 0.0), w=[GS[0]])
            groups = [(g0, min(g0 + GRP, NHC)) for g0 in range(0, NHC, GRP)]

            def up_slot(s_):
                return W3[:, s_ * 2048:(s_ + 1) * 2048]

            def dn_slot(s_):
                return W4[:, s_ * 4096:(s_ + 1) * 4096]

            def ysb(s_):
                return Ef32[:, s_ * TT:(s_ + 1) * TT]

            def ssb(s_):
                return Ef32[:, (2 + s_) * TT:(3 + s_) * TT]

            def dn_dma(gi):
                g0, g1 = groups[gi]
                s_ = gi % 2
                n = g1 - g0
                wdma(dn_slot(s_)[:, 0:n * 1024].rearrange("p (j n) -> p j n", j=n),
                     wdn_d[l, g0:g1].rearrange("j p n -> p j n"), [WDN[s_]], "wdn%d" % s_)

            for j in range(3):
                wdma(up_slot(j), wup_d[l, j], [WUP[j]], "wup%d" % j)
            dn_dma(0)
            dn_dma(1)
            cnt = [0]

            def up(gi):
                g0, g1 = groups[gi]
                for j in range(g0, g1):
                    s_ = j % 3
                    row = (gi % 2) * 4 + (j - g0)
                    wv = up_slot(s_).rearrange("p (k t n) -> p k t n", k=8, t=2)
                    for tt in range(NTT):
                        bg, bv = allb.next(), allb.next()
                        for which, bank in ((0, bg), (1, bv)):
                            mm_group(bank, [(banks[bank][:, :], wv[:, k, which, :], As(k, tt)) for k in range(8)],
                                     [WUP[s_]] + [RA[k][tt] for k in range(8)])
                        t_ = cnt[0] % 2
                        cnt[0] += 1
                        add("act", lambda e, bg=bg, tt=tt: e.copy(
                            out=Dg[:, 2 + tt * TT: 2 + (tt + 1) * TT], in_=banks[bg][:, :]),
                            r=[PS[bg]], w=[GS[tt]])
                        rd = [GS[tt]] + ([GS[tt - 1]] if tt > 0 else [])
                        cw = C_FCW + l * 66
                        add("dve", lambda e, t_=t_, tt=tt, j=j, cw=cw: e.tensor_scalar(
                            out=ysb(t_), in0=Dg[:, tt * TT:(tt + 1) * TT], scalar1=ccol(cw + j), scalar2=None,
                            op0=ALU.mult), r=rd + [CST], w=[YS[t_]])
                        for tap in (1, 2):
                            add("dve", lambda e, t_=t_, tt=tt, j=j, tap=tap, cw=cw: e.scalar_tensor_tensor(
                                out=ysb(t_), in0=Dg[:, tap + tt * TT: tap + (tt + 1) * TT],
                                scalar=ccol(cw + tap * NHC + j), in1=ysb(t_), op0=ALU.mult, op1=ALU.add),
                                r=rd + [CST, YS[t_]], w=[YS[t_]])
                        add("act", lambda e, t_=t_, j=j: e.activation(
                            out=ssb(t_), in_=ysb(t_), func=AF.Silu, bias=ccol(C_FCB + l * NHC + j)),
                            r=[YS[t_], CST], w=[SS[t_]])
                        add("dve", lambda e, t_=t_, bv=bv, row=row, tt=tt: e.tensor_tensor(
                            out=Bs(row, tt), in0=banks[bv][:, :], in1=ssb(t_), op=ALU.mult),
                            r=[PS[bv], SS[t_]], w=[AB[row][tt]])
                    if j + 3 < NHC:
                        wdma(up_slot(s_), wup_d[l, j + 3], [WUP[s_]], "wup%d" % s_)

            def down(gi):
                g0, g1 = groups[gi]
                s_ = gi % 2
                n = g1 - g0
                wv = dn_slot(s_).rearrange("p (j n) -> p j n", j=4)
                for tt in range(NTT):
                    for oc in range(8):
                        bank = allb.next()
                        mm_group(bank, [(banks[bank][:, :], wv[:, jj, oc * 128:(oc + 1) * 128],
                                         Bs((gi % 2) * 4 + jj, tt)) for jj in range(n)],
                                 [WDN[s_]] + [AB[(gi % 2) * 4 + jj][tt] for jj in range(n)])
                        add("dve", lambda e, bank=bank, oc=oc, tt=tt: e.tensor_tensor(
                            out=hs(oc, tt), in0=banks[bank][:, :], in1=hs(oc, tt), op=ALU.add),
                            r=[PS[bank], H[oc][tt]], w=[H[oc][tt]])
                if gi + 2 < len(groups):
                    dn_dma(gi + 2)

            up(0)
            for gi in range(1, len(groups)):
                up(gi)
                if gi == len(groups) - 1 and hook is not None:
                    hook(dict(WUP=WUP, GS=GS, YS=YS, SS=SS), old_res)
                down(gi - 1)
            down(len(groups) - 1)
            return sum(AB, []) + GS + YS + SS + WUP + WDN

        def rope_args(old_res):
            TB = grid("TB", 4)
            TMP = grid("TMP", 4)
            handoff(old_res, TB + TMP)
            T = Dg[:, 0:S]

            def tmp(i):
                return Ef32[:, i * TT:(i + 1) * TT]
            C1 = 6.28125
            C2 = TWO_PI - C1
            lim = 3.1415925
            for tt in range(NTT):
                sl = slice(tt * TT, (tt + 1) * TT)
                add("sp", lambda e, sl=sl: e.dma_start(out=Ei32[:, 3 * TT:4 * TT], in_=pos_d[:, sl]),
                    w=[TMP[3]], dma="pos")
                add("dve", lambda e: e.tensor_scalar(out=tmp(0), in0=Ei32[:, 3 * TT:4 * TT], scalar1=ccol(C_INVF),
                                                     scalar2=None, op0=ALU.mult), r=[TMP[3], CST], w=[TMP[0]])
                add("dve", lambda e: e.tensor_scalar(out=Ei32[:, 1 * TT:2 * TT], in0=tmp(0), scalar1=1.0 / TWO_PI,
                                                     scalar2=None, op0=ALU.mult), r=[TMP[0]], w=[TMP[1]])
                add("dve", lambda e: e.tensor_copy(out=tmp(2), in_=Ei32[:, 1 * TT:2 * TT]), r=[TMP[1]], w=[TMP[2]])
                add("dve", lambda e: e.scalar_tensor_tensor(out=tmp(0), in0=tmp(2), scalar=-C1, in1=tmp(0),
                                                            op0=ALU.mult, op1=ALU.add), r=[TMP[2], TMP[0]], w=[TMP[0]])
                add("dve", lambda e: e.scalar_tensor_tensor(out=tmp(0), in0=tmp(2), scalar=-C2, in1=tmp(0),
                                                            op0=ALU.mult, op1=ALU.add), r=[TMP[2], TMP[0]], w=[TMP[0]])
                add("dve", lambda e: e.tensor_single_scalar(out=tmp(2), in_=tmp(0), scalar=math.pi, op=ALU.is_gt),
                    r=[TMP[0]], w=[TMP[2]])
                add("dve", lambda e: e.scalar_tensor_tensor(out=tmp(0), in0=tmp(2), scalar=-TWO_PI, in1=tmp(0),
                                                            op0=ALU.mult, op1=ALU.add), r=[TMP[2], TMP[0]], w=[TMP[0]])
                add("dve", lambda e, sl=sl: e.tensor_scalar(out=T[:, sl], in0=tmp(0), scalar1=-lim, scalar2=lim,
                                                            op0=ALU.max, op1=ALU.min), r=[TMP[0]], w=[TB[tt]])
                add("dve", lambda e, sl=sl: e.scalar_tensor_tensor(out=T[0:64, sl], in0=T[0:64, sl], scalar=-1.0,
                                                                   in1=T[0:64, sl], op0=ALU.mult, op1=ALU.min),
                    r=[TB[tt]], w=[TB[tt]])
            return TB, TMP

        def rope_sin(TB):
            for tt in range(NTT):
                sl = slice(tt * TT, (tt + 1) * TT)
                add("act", lambda e, sl=sl: e.activation(out=Dg[:, sl], in_=Dg[:, sl], func=AF.Sin,
                                                         bias=ccol(C_BIAS), scale=ccol(C_SIGN)),
                    r=[TB[tt], CST], w=[TB[tt]])

        early = {}

        def early_hook(loc, mixer_res):
            WKV = grid("WKV", 3)
            WA = grid("WA", 4)
            handoff(loc["WUP"] + [r_ for r_ in mixer_res], WKV + WA)
            wdma(W3[:, 0:2048], wdkv_d[:, :], [WKV[0]], "wkv0")
            wdma(W3[:, 2048:5120], wdq_d[:, :], [WKV[1]], "wkv1")
            wdma(W3[:, 5120:6144], wkr_d[:, :], [WKV[2]], "wkv2")
            wdma(W2[:, 0:2048], wuk_d[:, :], [WA[0]], "wa0")
            wdma(W2[:, 3072:5120], wuv_d[:, :], [WA[1]], "wa1")
            for hh in range(2):
                o = 2048 + hh * 3072
                wdma(W2[:, o:o + 768], wuq_d[hh], [WA[2 + hh]], "wa%d" % (2 + hh))
            TB, TMP = rope_args(loc["GS"] + loc["YS"] + loc["SS"])
            early.update(WKV=WKV, WA=WA, TB=TB, TMP=TMP)

        def kv_phase(old_res):
            TB, TMP, WKV = early["TB"], early["TMP"], early["WKV"]
            CKF = grid("CKF", 3, 4)
            CKV = grid("CKV", 2, 4)
            KR = grid("KR", 4)
            RT = grid("RT", 2)
            handoff(old_res, sum(CKF, []) + sum(CKV, []) + KR + RT)
            norm_h_to_A(C_KVIN)
            wdkv = W3[:, 0:2048].rearrange("p (k n) -> p k n", k=8)
            wdq = W3[:, 2048:5120].rearrange("p (k n) -> p k n", k=8)
            wkr = W3[:, 5120:6144].rearrange("p (k n) -> p k n", k=8)

            def ckf(c, tt):
                return Bf32[:, c * S + tt * TT: c * S + (tt + 1) * TT]

            def ckv(c, tt):
                return W4[:, c * S + tt * TT: c * S + (tt + 1) * TT]

            def kr(tt):
                return W4[0:64, 2 * S + tt * TT: 2 * S + (tt + 1) * TT]

            def krf(tt):
                return W4[:, 2 * S + tt * TT: 2 * S + (tt + 1) * TT]
            add("dve", lambda e: e.memset(W4[64:128, 2 * S:3 * S], 0.0), w=KR)

            for c in range(2):
                for tt in range(NTT):
                    bank = allb.next()
                    mm_group(bank, [(banks[bank][:, :], wdkv[:, k, c * 128:(c + 1) * 128], As(k, tt)) for k in range(8)],
                             [WKV[0]] + [RA[k][tt] for k in range(8)])
                    add("act", lambda e, bank=bank, c=c, tt=tt: e.copy(out=ckf(c, tt), in_=banks[bank][:, :]),
                        r=[PS[bank]], w=[CKF[c][tt]])
            rmsnorm(2, ckf, lambda c, tt: CKF[c][tt], C_KVLAT, ckv, lambda c, tt: CKV[c][tt], 256.0)
            KRAW = grid("KRAW", 4)
            handoff([], KRAW)

            def kraw(tt):
                return Bf32[:, 3 * S + tt * TT: 3 * S + (tt + 1) * TT]
            for tt in range(NTT):
                bank = allb.next()
                mm_group(bank, [(banks[bank][:, :], wkr[:, k, :], As(k, tt)) for k in range(8)],
                         [WKV[2]] + [RA[k][tt] for k in range(8)])
                add("act", lambda e, bank=bank, tt=tt: e.copy(out=kraw(tt), in_=banks[bank][:, :]),
                    r=[PS[bank]], w=[KRAW[tt]])

            def krope_finish():
                rope_sin(TB)
                for tt in range(NTT):
                    i = rope_cnt[0] % 2
                    rope_cnt[0] += 1
                    tb = E[:, 9728 + i * TT: 9728 + (i + 1) * TT]
                    add("dve", lambda e, tt=tt, tb=tb: e.tensor_tensor(
                        out=tb, in0=kraw(tt), in1=Dg[:, tt * TT:(tt + 1) * TT], op=ALU.mult),
                        r=[KRAW[tt], TB[tt]], w=[RT[i]])
                    b2 = allb.next()
                    add("pe", lambda e, b2=b2, tb=tb: e.matmul(banks[b2][:, :], Jm, tb, start=True, stop=True),
                        r=[RT[i], CST], w=[PS[b2]])
                    add("dve", lambda e, b2=b2, tt=tt: e.tensor_copy(out=kr(tt), in_=banks[b2][0:64, :]),
                        r=[PS[b2]], w=[KR[tt]])
            return dict(TB=TB, TMP=TMP, WKV=WKV, CKF=CKF, CKV=CKV, KR=KR, RT=RT, ckv=ckv, kr=krf, wdq=wdq,
                        KRAW=KRAW, krope_finish=krope_finish)

        rope_cnt = [0]

        def rope_fin(bank, tt, TB, RT, dst_ap, dst_res):
            i = rope_cnt[0] % 2
            rope_cnt[0] += 1
            tb = E[:, 9728 + i * TT: 9728 + (i + 1) * TT]
            add("dve", lambda e: e.tensor_tensor(out=tb, in0=banks[bank][:, :], in1=Dg[:, tt * TT:(tt + 1) * TT],
                                                 op=ALU.mult), r=[PS[bank], TB[tt]], w=[RT[i]])
            b2 = allb.next()
            add("pe", lambda e: e.matmul(banks[b2][:, :], Jm, tb, start=True, stop=True),
                r=[RT[i], CST], w=[PS[b2]])
            add("act", lambda e: e.copy(out=dst_ap, in_=banks[b2][0:64, :]), r=[PS[b2]], w=[dst_res])

        def mla(kv):
            TB, RT, CKV, KR, WKV, CKF = kv["TB"], kv["RT"], kv["CKV"], kv["KR"], kv["WKV"], kv["CKF"]
            ckv, kr, wdq = kv["ckv"], kv["kr"], kv["wdq"]
            norm_h_to_A(C_ATTN + 8)
            CQ = grid("CQ", 3, 4)
            PT = grid("PT", 7)
            handoff(kv["TMP"], sum(CQ, []) + PT)

            def cqf(c, tt):
                return Bf32[:, c * S + tt * TT: c * S + (tt + 1) * TT]

            def cq(c, tt):
                return E[:, c * S + tt * TT: c * S + (tt + 1) * TT]

            def pt(i):
                return E[:, 6144 + i * TT: 6144 + (i + 1) * TT]

            for c in range(3):
                for tt in range(NTT):
                    bank = allb.next()
                    mm_group(bank, [(banks[bank][:, :], wdq[:, k, c * 128:(c + 1) * 128], As(k, tt)) for k in range(8)],
                             [WKV[1]] + [RA[k][tt] for k in range(8)])
                    add("act", lambda e, bank=bank, c=c, tt=tt: e.copy(out=cqf(c, tt), in_=banks[bank][:, :]),
                        r=[PS[bank]], w=[CKF[c][tt]])
            rmsnorm(3, cqf, lambda c, tt: CKF[c][tt], C_QLAT, cq, lambda c, tt: CQ[c][tt], 384.0)
            kv["krope_finish"]()
            WO = grid("WO1", 8)
            handoff(WKV, WO)

            def wo31(oc):
                return W3[:, oc * 1024:(oc + 1) * 1024] if oc < 6 else W1[:, (oc - 6) * 1024:(oc - 5) * 1024]
            for oc in range(8):
                wdma(wo31(oc), wo_d[oc], [WO[oc]], "wo%d" % oc)

            HB = grid("HB", 2, 4, 4)
            WA = early["WA"]
            OT = RA
            handoff(sum(CKF, []) + kv["KRAW"], sum(sum(HB, []), []))
            wuk = W2[:, 0:2048].rearrange("p (k n) -> p k n", k=2)
            wuv = W2[:, 3072:5120].rearrange("p (k n) -> p k n", k=2)

            def wuq_slot(s_):
                o = 2048 + s_ * 3072
                return W2[:, o:o + 768]
            for s2 in range(2):
                add("dve", lambda e, s2=s2: e.memset(B[64:128, (4 * s2 + 2) * S:(4 * s2 + 3) * S], 0.0), w=HB[s2][2])
            for d in range(4):
                add("dve", lambda e, d=d: e.memset(pt(3 + d)[64:128, 128 * d:128 * d + 64], 0.0), w=[PT[3 + d]])

            sb_ = BankPool([0, 1, 2])
            ob_ = BankPool([3, 4])
            lb_ = BankPool([5, 6])
            mb_ = BankPool([7])
            pt_cnt = [0]

            def prep_pieces(hd):
                s_ = hd % 2
                wq = wuq_slot(s_).rearrange("p (k n) -> p k n", k=3)
                pieces = []

                def p_qn(tt):
                    bank = mb_.next()
                    mm_group(bank, [(banks[bank][:, :], wq[:, k, 0:128], cq(k, tt)) for k in range(3)],
                             [WA[2 + s_]] + [CQ[k][tt] for k in range(3)])
                    add("dve", lambda e: e.tensor_copy(out=Bs(4 * s_ + 0, tt), in_=banks[bank][:, :]),
                        r=[PS[bank]], w=[HB[s_][0][tt]])

                def p_k(tt):
                    bank = mb_.next()
                    mm_group(bank, [(banks[bank][:, :], wuk[:, k, hd * 128:(hd + 1) * 128], ckv(k, tt)) for k in range(2)],
                             [WA[0]] + [CKV[k][tt] for k in range(2)])
                    add("dve", lambda e: e.tensor_copy(out=Bs(4 * s_ + 1, tt), in_=banks[bank][:, :]),
                        r=[PS[bank]], w=[HB[s_][1][tt]])

                qr_tb = {}

                def p_qr(tt):
                    bank = mb_.next()
                    mm_group(bank, [(banks[bank][:, :], wq[:, k, 128:256], cq(k, tt)) for k in range(3)],
                             [WA[2 + s_]] + [CQ[k][tt] for k in range(3)])
                    i = rope_cnt[0] % 2
                    rope_cnt[0] += 1
                    tb = E[:, 9728 + i * TT: 9728 + (i + 1) * TT]
                    qr_tb[tt] = (i, tb)
                    add("dve", lambda e: e.tensor_tensor(out=tb, in0=banks[bank][:, :], in1=Dg[:, tt * TT:(tt + 1) * TT],
                                                         op=ALU.mult), r=[PS[bank], TB[tt]], w=[RT[i]])

                def p_qr2(tt):
                    i, tb = qr_tb[tt]
                    b2 = mb_.next()
                    add("pe", lambda e: e.matmul(banks[b2][:, :], Jm, tb, start=True, stop=True),
                        r=[RT[i], CST], w=[PS[b2]])
                    add("dve", lambda e: e.tensor_copy(out=Bs(4 * s_ + 2, tt)[0:64, :], in_=banks[b2][0:64, :]),
                        r=[PS[b2]], w=[HB[s_][2][tt]])

                def p_v(tt):
                    bank = mb_.next()
                    pairs = []
                    for kb4 in range(4):
                        for k in range(2):
                            pairs.append((banks[bank][:, kb4 * 128:(kb4 + 1) * 128],
                                          ckv(k, tt)[:, kb4 * 128:(kb4 + 1) * 128],
                                          wuv[:, k, hd * 128:(hd + 1) * 128], k))

                    def fn(e):
                        ins = None
                        for (o, l, r, k) in pairs:
                            ins = e.matmul(o, l, r, start=(k == 0), stop=(k == 1))
                        return ins
                    add("pe", fn, r=[WA[1]] + [CKV[k][tt] for k in range(2)], w=[PS[bank]])
                    add("dve", lambda e: e.tensor_copy(out=Bs(4 * s_ + 3, tt), in_=banks[bank][:, :]),
                        r=[PS[bank]], w=[HB[s_][3][tt]])

                for tt in range(NTT):
                    for f in (p_qr, p_qn, p_k, p_qr2, p_v):
                        pieces.append(lambda f=f, tt=tt: f(tt))

                def p_dma():
                    if hd + 2 < NHEAD:
                        wdma(wuq_slot(s_), wuq_d[hd + 2], [WA[2 + s_]], "wa%d" % (2 + s_))
                pieces.append(p_dma)
                return pieces

            def bufs(hd):
                s_ = hd % 2
                qn = lambda c0, c1: B[:, (4 * s_ + 0) * S + c0:(4 * s_ + 0) * S + c1]
                Kh = lambda c0, c1: B[:, (4 * s_ + 1) * S + c0:(4 * s_ + 1) * S + c1]
                qr = lambda c0, c1: B[:, (4 * s_ + 2) * S + c0:(4 * s_ + 2) * S + c1]
                Vh = lambda kb: B[:, (4 * s_ + 3) * S + kb * 128:(4 * s_ + 3) * S + (kb + 1) * 128]
                return qn, Kh, qr, Vh

            def emit_S(hd, qt, kb):
                s_ = hd % 2
                qn, Kh, qr, Vh = bufs(hd)
                q0 = qt * TT
                d = kb - 4 * qt
                c0 = 128 * d if d > 0 else 0
                sbk = sb_.next()
                ktt = kb // 4
                mm_group(sbk, [(banks[sbk][:, c0:TT], Kh(kb * 128, (kb + 1) * 128), qn(q0 + c0, q0 + TT)),
                               (banks[sbk][:, c0:TT], kr(ktt)[:, (kb % 4) * 128:(kb % 4 + 1) * 128],
                                qr(q0 + c0, q0 + TT))],
                         [HB[s_][1][ktt], HB[s_][0][qt], KR[ktt], HB[s_][2][qt]])
                return sbk

            cur = {}

            def emit_rest(hd, qt, kb, sbk):
                s_ = hd % 2
                qn, Kh, qr, Vh = bufs(hd)
                nkb = 4 * qt + 4
                d = kb - 4 * qt
                c0 = 128 * d if d > 0 else 0
                ktt = kb // 4
                if kb == 0:
                    cur["ob"], cur["lb"] = ob_.next(), lb_.next()
                ob, lb = cur["ob"], cur["lb"]
                if d < 0:
                    pi_ = pt_cnt[0] % 3
                    pt_cnt[0] += 1
                    add("act", lambda e: e.activation(
                        out=pt(pi_), in_=banks[sbk][:, :], func=AF.Exp, scale=SCALE),
                        r=[PS[sbk]], w=[PT[pi_]])
                else:
                    pi_ = 3 + d
                    add("act", lambda e: e.activation(
                        out=pt(pi_)[:, c0 + 64:TT], in_=banks[sbk][:, c0 + 64:TT], func=AF.Exp, scale=SCALE),
                        r=[PS[sbk]], w=[PT[pi_]])
                    add("act", lambda e: e.activation(
                        out=pt(pi_)[0:64, c0:c0 + 64], in_=banks[sbk][0:64, c0:c0 + 64], func=AF.Exp, scale=SCALE),
                        r=[PS[sbk]], w=[PT[pi_]])
                first, last = (kb == 0), (kb == nkb - 1)
                add("pe", lambda e: e.matmul(
                    banks[ob][:, c0:TT], Vh(kb), pt(pi_)[:, c0:TT], start=first, stop=last),
                    r=[HB[s_][3][ktt], PT[pi_]], w=[PS[ob]])
                add("pe", lambda e: e.matmul(
                    banks[lb][:, c0:TT], ones, pt(pi_)[:, c0:TT], start=first, stop=last),
                    r=[CST, PT[pi_]], w=[PS[lb]])
                if last:
                    for ch in range(4):
                        todo.append(lambda ch=ch: add("dve", lambda e: e.reciprocal(
                            out=rs2[:, ch * 128:(ch + 1) * 128], in_=banks[lb][:, ch * 128:(ch + 1) * 128]),
                            r=[PS[lb]], w=SQ))
                    todo.append(lambda: add("dve", lambda e: e.tensor_tensor(
                        out=As(hd, qt), in0=banks[ob][:, :], in1=rs2[:, :], op=ALU.mult),
                        r=[PS[ob]] + SQ, w=[OT[hd][qt]]))

            todo = []
            for p in prep_pieces(0):
                p()
            steps = [(hd, qt, kb) for hd in range(NHEAD) for qt in range(NTT) for kb in range(4 * qt + 4)]
            LOOK = 2
            pieces = []
            sq_ = []
            for j in range(LOOK):
                sq_.append(emit_S(*steps[j]))
            for i, (hd, qt, kb) in enumerate(steps):
                if qt == 0 and kb == 0:
                    for p in pieces:
                        p()
                    pieces = prep_pieces(hd + 1) if hd + 1 < NHEAD else []
                sbk = sq_.pop(0)
                if i + LOOK < len(steps):
                    if steps[i + LOOK][0] != hd:
                        for p in pieces:
                            p()
                        pieces = []
                    sq_.append(emit_S(*steps[i + LOOK]))
                if pieces and i % 2 == 1:
                    pieces.pop(0)()
                emit_rest(hd, qt, kb, sbk)
                if todo:
                    todo.pop(0)()
            for t_ in todo:
                t_()

            proj_out(wo_d, wo31, WO, As, lambda k, tt: OT[k][tt], issue=False)
            return (sum(CQ, []) + PT + sum(sum(HB, []), []) + WA + WO + TB + RT + sum(CKV, []) + KR + WKV)

        def final(do_norm):
            if do_norm:
                rmsnorm(8, hs, lambda c, tt: H[c][tt], C_FINAL, hs, lambda c, tt: H[c][tt], float(D))
            for tt in range(NTT):
                add("sp", lambda e, tt=tt: e.dma_start(out=o3[:, :, tt * TT:(tt + 1) * TT], in_=h3[:, :, tt * TT:(tt + 1) * TT]),
                    r=[H[c][tt] for c in range(8)], w=[OUT[tt]], dma="o%d" % tt)
            add("sp", None, r=OUT, w=[])

        stages = ["mixer0", "ffn0", "kv", "mla", "ffn1"]
        n_st = len(stages) if stop_after is None else stages.index(stop_after) + 1
        res = []
        kv = None
        if n_st >= 1:
            res = mixer0()
        if n_st >= 2:
            res = ffn(0, res, hook=early_hook)
        if n_st >= 3:
            kv = kv_phase(res)
        if n_st >= 4:
            res = mla(kv)
        if n_st >= 5:
            res = ffn(1, res)
        final(final_norm and stop_after is None)

        with nc.Block() as block:
            prog.emit(nc, block, stack)
    nc._prog_stats = prog.stats
    return nc


def _kchunk(w):
    K, N = w.shape
    return np.ascontiguousarray(w.reshape(K // 128, 128, N).transpose(1, 0, 2))


def _prep_shared(inp):
    f = np.float32
    sh = {}
    cst = np.zeros((128, C_NCOL), f)

    def put(col, vec):
        n = vec.shape[0] // 128
        cst[:, col:col + n] = vec.reshape(n, 128).T
    for l in range(2):
        put(C_ATTN + l * 8, inp["attn_norm"][l])
        put(C_FFN + l * 8, inp["ffn_norm"][l])
    put(C_FINAL, inp["final_norm"])
    put(C_KVIN, inp["kv_in_norm"])
    put(C_KVLAT, inp["kv_latent_norm"])
    put(C_QLAT, inp["q_latent_norm"][0])
    for tap in range(3):
        put(C_SCW + tap * 8, inp["sc_conv_w"][0, tap])
        for l in range(2):
            put(C_FCW + l * 66 + tap * NHC, inp["ffn_conv_w"][l, tap])
    for l in range(2):
        put(C_FCB + l * NHC, inp["ffn_conv_b"][l])
    half = 32
    inv_freq = (1.0 / (np.float32(10000.0) ** (np.arange(half, dtype=f) / np.float32(half)))).astype(f)
    cst[:, C_INVF] = inv_freq[np.arange(128) % 32]
    sign = np.ones(128, f)
    sign[64:96] = -1.0
    cst[:, C_SIGN] = sign
    cst[0:64, C_BIAS] = np.float32(math.pi / 2)
    sh["cst"] = cst
    cb = np.zeros((128, 256), f)
    cb[:, 0:128] = 1.0
    pp = np.arange(128)
    cb[:, 128:256] = (pp[:, None] % 64 == pp[None, :] % 64).astype(f)
    sh["cb"] = cb
    w_in = _kchunk(inp["sc_w_in"][0])
    w_in = w_in.reshape(128, 8, 3, 8, 128).transpose(3, 0, 1, 2, 4)
    sh["w_in"] = np.ascontiguousarray(w_in).reshape(8, 128, 8 * 3 * 128)

    def panels(w):
        t = _kchunk(w).reshape(128, 8, 8, 128).transpose(2, 0, 1, 3)
        return np.ascontiguousarray(t).reshape(8, 128, 8 * 128)
    sh["w_out"] = panels(inp["sc_w_out"][0])
    sh["w_o"] = panels(inp["w_o"][0])
    wup = np.stack([_kchunk(inp["ffn_w_up"][l]) for l in range(2)])
    wup = wup.reshape(2, 128, 8, 2, NHC, 128).transpose(0, 4, 1, 2, 3, 5)
    sh["w_up"] = np.ascontiguousarray(wup).reshape(2, NHC, 128, 8 * 2 * 128)
    sh["w_down"] = np.ascontiguousarray(inp["ffn_w_down"].reshape(2, NHC, 128, 1024))
    sh["w_dkv"] = _kchunk(inp["w_dkv"]).reshape(128, 8 * 256)
    wkr = inp["w_kr"]
    wkr2 = np.concatenate([wkr, wkr[:, 32:64], wkr[:, 0:32]], axis=1)
    sh["w_kr"] = _kchunk(wkr2).reshape(128, 8 * 128)
    sh["w_uk"] = _kchunk(inp["w_uk"]).reshape(128, 2 * 1024)
    sh["w_uv"] = _kchunk(inp["w_uv"]).reshape(128, 2 * 1024)
    sh["w_dq"] = _kchunk(inp["w_dq"][0]).reshape(128, 8 * 384)
    wuq = _kchunk(inp["w_uq"][0]).reshape(128, 3, 8, 192)
    wuq2 = np.concatenate([wuq, wuq[..., 160:192], wuq[..., 128:160]], axis=-1)
    sh["w_uq"] = np.ascontiguousarray(wuq2.transpose(2, 0, 1, 3)).reshape(8, 128, 3 * 256)
    return sh


def _prep_core(inp, b):
    x = inp["x"][b]
    xT = np.ascontiguousarray(x.T.reshape(8, 128, S).transpose(1, 0, 2)).reshape(128, 8 * S)
    pos = np.ascontiguousarray(np.broadcast_to(inp["positions"][b].astype(np.int32)[None, :], (128, S)))
    return {"xT": xT, "pos": pos}


_NC_CACHE = {}


def run(inputs, stop_after=None, final_norm=True, trace=False):
    inp = {k: np.asarray(v) for k, v in inputs.items()}
    key = (stop_after, final_norm)
    if key not in _NC_CACHE:
        _NC_CACHE[key] = build(stop_after, final_norm)
    nc = _NC_CACHE[key]
    sh = _prep_shared(inp)
    in_maps = []
    for b in range(8):
        m = dict(sh)
        m.update(_prep_core(inp, b))
        in_maps.append(m)
    res = run_bass_kernel_spmd(nc, in_maps, core_ids=list(range(8)), trace=trace)
    outs = []
    for b in range(8):
        oT = np.asarray(res.results[b]["outT"]).reshape(128, 8, S)
        outs.append(oT.transpose(2, 1, 0).reshape(S, D))
    return np.stack(outs).astype(np.float32), res


def kernel(**inputs):
    out, _ = run(inputs)
    return out
```
